# Optimizing a Trainium2 kernel written in Bass

```python
import jax, jax.numpy as jnp
from jax import lax
import numpy as np

D_MODEL = 1024
BATCH = 4
SEQ = 8192
DEPTH = 1

HEAD_DIM = 64
N_HEADS_FOX = D_MODEL // (2 * HEAD_DIM)
N_HEADS_SB = D_MODEL // (2 * HEAD_DIM)
D_FOX = N_HEADS_FOX * HEAD_DIM
D_SB = N_HEADS_SB * HEAD_DIM
D_MIX = D_FOX + D_SB
N_IN = 3 * D_FOX + 3 * D_SB + N_HEADS_FOX
BLOCK_Q = 128
D_FF = int(round(8 * D_MODEL / 3 / 64)) * 64
CONV_WIDTH = 3
N_MOD = 6
EPS = 1e-6

kernel_name = 'hybrid_fox_stickbreaking_convffn_adaln'


def rms_norm(x, g):
    xf = x.astype(jnp.float32)
    y = xf * lax.rsqrt(jnp.mean(xf * xf, axis=-1, keepdims=True) + EPS)
    return (y * g.astype(jnp.float32)).astype(x.dtype)


def split_heads(t, n_heads):
    B, S, _ = t.shape
    return t.reshape(B, S, n_heads, HEAD_DIM).transpose(0, 2, 1, 3)


def head_rms_norm(o, g):
    B, H, S, Dh = o.shape
    o = o.transpose(0, 2, 1, 3)
    return rms_norm(o, g.reshape(H, Dh)).reshape(B, S, H * Dh)


def to_blocks(t):
    B, H, S = t.shape[:3]
    nb = S // BLOCK_Q
    t = t.reshape((B, H, nb, BLOCK_Q) + t.shape[3:])
    return jnp.moveaxis(t, 2, 0)


def from_blocks(t):
    nb, B, H, bq, Dh = t.shape
    return jnp.moveaxis(t, 0, 2).reshape(B, H, nb * bq, Dh)


def forgetting_attention(q, k, v, log_f):
    S = q.shape[2]
    scale = HEAD_DIM ** -0.5
    F = jnp.cumsum(log_f.astype(jnp.float32), axis=-1)
    kpos = jnp.arange(S)
    nb = S // BLOCK_Q

    def one_block(args):
        i, q_blk, F_blk = args
        qpos = i * BLOCK_Q + jnp.arange(BLOCK_Q)
        s = jnp.einsum('bhqd,bhkd->bhqk', q_blk, k, preferred_element_type=jnp.float32) * scale
        s = s + F_blk[..., None] - F[:, :, None, :]
        causal = kpos[None, :] <= qpos[:, None]
        s = jnp.where(causal, s, -jnp.inf)
        p = jax.nn.softmax(s, axis=-1)
        return jnp.einsum('bhqk,bhkd->bhqd', p.astype(v.dtype), v)

    out = lax.map(one_block, (jnp.arange(nb), to_blocks(q), to_blocks(F)))
    return from_blocks(out)


def stick_breaking_attention(q, k, v):
    S = q.shape[2]
    scale = HEAD_DIM ** -0.5
    kpos = jnp.arange(S)
    nb = S // BLOCK_Q

    def one_block(args):
        i, q_blk = args
        qpos = i * BLOCK_Q + jnp.arange(BLOCK_Q)
        z = jnp.einsum('bhqd,bhkd->bhqk', q_blk, k, preferred_element_type=jnp.float32) * scale
        strict = kpos[None, :] < qpos[:, None]
        log_one_minus_beta = jnp.where(strict, jax.nn.log_sigmoid(-z), 0.0)
        rest = lax.cumsum(log_one_minus_beta, axis=3, reverse=True) - log_one_minus_beta
        log_a = jax.nn.log_sigmoid(z) + rest
        a = jnp.where(strict, jnp.exp(log_a), 0.0)
        return jnp.einsum('bhqk,bhkd->bhqd', a.astype(v.dtype), v)

    out = lax.map(one_block, (jnp.arange(nb), to_blocks(q)))
    return from_blocks(out)


def causal_depthwise_conv(u, w, b):
    C = u.shape[-1]
    y = lax.conv_general_dilated(
        u, w.astype(u.dtype).reshape(CONV_WIDTH, 1, C),
        window_strides=(1,), padding=[(CONV_WIDTH - 1, 0)],
        dimension_numbers=('NWC', 'WIO', 'NWC'), feature_group_count=C)
    return y + b.astype(u.dtype)


def setup_inputs(seed: int = 0) -> dict:
    key = jax.random.key(seed)
    ks = jax.random.split(key, 16)
    L, D = DEPTH, D_MODEL
    f32 = jnp.float32

    def nrm(k, shape, s):
        return jax.random.normal(k, shape, f32) * s

    return {
        'x': nrm(ks[0], (BATCH, SEQ, D), 1.0),
        'c': nrm(ks[1], (BATCH, D), 1.0),
        'w_ada': nrm(ks[2], (L, D, N_MOD * D), D ** -0.5),
        'b_ada': nrm(ks[3], (L, N_MOD * D), 0.02),
        'g_attn': 1.0 + nrm(ks[4], (L, D), 0.02),
        'w_in': nrm(ks[5], (L, D, N_IN), D ** -0.5),
        'b_fgate': 2.0 + nrm(ks[6], (L, N_HEADS_FOX), 0.5),
        'g_out_fox': 1.0 + nrm(ks[7], (L, D_FOX), 0.02),
        'g_out_sb': 1.0 + nrm(ks[8], (L, D_SB), 0.02),
        'w_out': nrm(ks[9], (L, D_MIX, D), D_MIX ** -0.5),
        'g_mlp': 1.0 + nrm(ks[10], (L, D), 0.02),
        'w_up': nrm(ks[11], (L, D, 2 * D_FF), D ** -0.5),
        'conv_w': nrm(ks[12], (L, CONV_WIDTH, 2 * D_FF), CONV_WIDTH ** -0.5),
        'conv_b': nrm(ks[13], (L, 2 * D_FF), 0.02),
        'w_down': nrm(ks[14], (L, D_FF, D), D_FF ** -0.5),
        'g_final': 1.0 + nrm(ks[15], (D,), 0.02),
    }


def reference(x, c, w_ada, b_ada, g_attn, w_in, b_fgate, g_out_fox, g_out_sb, w_out,
              g_mlp, w_up, conv_w, conv_b, w_down, g_final):
    sizes = [D_FOX, D_FOX, D_FOX, D_SB, D_SB, D_SB]
    offsets = np.cumsum(sizes).tolist()
    for l in range(DEPTH):
        mod = jax.nn.silu(c) @ w_ada[l] + b_ada[l]
        shift_a, scale_a, gate_a, shift_m, scale_m, gate_m = [
            m[:, None, :] for m in jnp.split(mod, N_MOD, axis=-1)]

        h = rms_norm(x, g_attn[l]) * (1.0 + scale_a) + shift_a
        proj = h @ w_in[l]
        q_f, k_f, v_f, q_s, k_s, v_s, f_logit = jnp.split(proj, offsets, axis=-1)
        log_f = jax.nn.log_sigmoid((f_logit + b_fgate[l]).astype(jnp.float32))
        o_fox = forgetting_attention(split_heads(q_f, N_HEADS_FOX), split_heads(k_f, N_HEADS_FOX),
                                     split_heads(v_f, N_HEADS_FOX), log_f.transpose(0, 2, 1))
        o_sb = stick_breaking_attention(split_heads(q_s, N_HEADS_SB), split_heads(k_s, N_HEADS_SB),
                                        split_heads(v_s, N_HEADS_SB))
        mix = jnp.concatenate([head_rms_norm(o_fox, g_out_fox[l]),
                               head_rms_norm(o_sb, g_out_sb[l])], axis=-1)
        x = x + gate_a * (mix @ w_out[l])

        h = rms_norm(x, g_mlp[l]) * (1.0 + scale_m) + shift_m
        u = causal_depthwise_conv(h @ w_up[l], conv_w[l], conv_b[l])
        u_gate, u_val = jnp.split(u, 2, axis=-1)
        x = x + gate_m * ((jax.nn.silu(u_gate) * u_val) @ w_down[l])
    return rms_norm(x, g_final)
```

```python
import contextlib
import numpy as np
import concourse.bass as bass
import concourse.mybir as mybir
from concourse.bass_utils import run_bass_kernel_spmd

F32 = mybir.dt.float32
BF16 = mybir.dt.bfloat16
AF = mybir.ActivationFunctionType
ALU = mybir.AluOpType

D = 1024
HD = 64
NH = 16
N_IN = 3080
DFF = 2752
EPS = 1e-6
NG = 22

ENGS = ('pe', 'act', 'dve', 'pool', 'sp')
NDS = 6


class Buf:
    __slots__ = ('w', 'r', 'g', 'name')

    def __init__(self, name=''):
        self.w = []
        self.r = []
        self.g = []
        self.name = name


class Rec:
    __slots__ = ('eng', 'fn', 'deps', 'is_dma', 'sig', 'sem', 'val', 'prewait', 'phase')


class Prog:
    def __init__(self, nc):
        self.nc = nc
        self.phase = 0
        self.lists = {e: [] for e in ENGS}

    def op(self, eng, fn, reads=(), writes=(), dma=False):
        rec = Rec()
        rec.eng = eng
        rec.fn = fn
        rec.is_dma = dma
        rec.sig = dma
        rec.sem = None
        rec.val = 0
        rec.prewait = None
        rec.phase = self.phase
        d2 = []
        seen = set()

        def add(d):
            if d is None or d.phase != self.phase or id(d) in seen:
                return
            if d.eng == 'pe' and eng == 'pe':
                return
            seen.add(id(d))
            d2.append(d)
        for b in reads:
            for wr in b.w:
                add(wr)
        par = {}
        for b in writes:
            p = dma and len(b.w) > 0 and all(wr.is_dma for wr in b.w) and not b.r
            par[id(b)] = p
            if p:
                for g in b.g:
                    add(g)
            else:
                for wr in b.w:
                    add(wr)
                for r in b.r:
                    add(r)
        rec.deps = d2
        for d in d2:
            d.sig = True
        for b in writes:
            if par[id(b)]:
                b.w.append(rec)
            else:
                b.g = list(b.w) + list(b.r)
                b.w = [rec]
                b.r = []
        for b in reads:
            b.r.append(rec)
        self.lists[eng].append(rec)
        return rec

    def run_phase(self):
        nc = self.nc
        lists = self.lists
        ph = self.phase
        with contextlib.ExitStack() as es:
            csem = {e: es.enter_context(nc.semaphore(f"c_{e}_{ph}")) for e in ENGS}
            dsem = {e: ([es.enter_context(nc.semaphore(f"d_{e}_{i}_{ph}")) for i in range(NDS)]
                        if any(r.is_dma for r in lists[e]) else [])
                    for e in ENGS}
            final_tick = {}
            for e in ENGS:
                last = None
                for r in lists[e]:
                    if not r.is_dma:
                        last = r
                if last is not None:
                    last.sig = True
            dcount = {e: 0 for e in ENGS}
            for e in ENGS:
                t = 0
                for r in lists[e]:
                    if r.is_dma:
                        n = dcount[e]
                        s = n % NDS
                        r.sem = dsem[e][s]
                        r.val = 16 * (n // NDS + 1)
                        r.prewait = (dsem[e][s], 16 * (n // NDS)) if n >= NDS else None
                        dcount[e] = n + 1
                    elif r.sig:
                        t += 1
                        r.sem = csem[e]
                        r.val = t
                final_tick[e] = t

            def replay(e, h):
                known = {}

                def wait(sem, val):
                    k = id(sem)
                    if known.get(k, 0) >= val:
                        return
                    known[k] = val
                    h.wait_ge(sem, val)
                for r in lists[e]:
                    for d in r.deps:
                        wait(d.sem, d.val)
                    if r.prewait is not None:
                        wait(*r.prewait)
                    ins = r.fn(h)
                    if r.sig:
                        ins.then_inc(r.sem, 16 if r.is_dma else 1)
                for e2 in ENGS:
                    if final_tick[e2] > 0:
                        wait(csem[e2], final_tick[e2])
                    n = dcount[e2]
                    for s in range(min(n, NDS)):
                        cnt = (n - s + NDS - 1) // NDS
                        wait(dsem[e2][s], 16 * cnt)

            with nc.Block() as block:
                @block.tensor
                def _(h):
                    replay('pe', h)

                @block.scalar
                def _(h):
                    replay('act', h)

                @block.vector
                def _(h):
                    replay('dve', h)

                @block.gpsimd
                def _(h):
                    replay('pool', h)

                @block.sync
                def _(h):
                    replay('sp', h)
        self.lists = {e: [] for e in ENGS}
        self.phase += 1


class Ctx:
    _n = [0]

    def __init__(self, nc, es):
        self.nc = nc
        self.es = es

    def sb(self, shape, dt, name=None):
        Ctx._n[0] += 1
        nm = f"t{Ctx._n[0]}"
        t = self.es.enter_context(self.nc.sbuf_tensor(nm, list(shape), dt))
        return t, Buf(nm)

    def ps(self, dt=F32, name=None):
        Ctx._n[0] += 1
        nm = f"p{Ctx._n[0]}"
        cols = 512 if dt == F32 else 1024
        t = self.es.enter_context(self.nc.psum_tensor(nm, [128, cols], dt))
        return t, Buf(nm)


def build(S, nph=99):
    NT = S // 512
    NB = S // 128
    NSL = NT // 2
    SO = NSL * 512
    SOH = SO + 128
    NHQ = 2 * NSL
    nc = bass.Bass("TRN2", target_bir_lowering=False)

    def din(name, shape, dt=F32):
        return nc.dram_tensor(name, list(shape), dt, kind="ExternalInput").ap()

    def dscr(name, shape, dt):
        import os
        dbg = os.environ.get("MK_DBG", "").split(",")
        return nc.dram_tensor(name, list(shape), dt, kind="ExternalOutput" if name in dbg else "Internal").ap()

    x = din("x", [S, D])
    xo = din("xo", [SOH, D])
    selw = din("selw", [8, 2])
    hvalid = din("hvalid", [128, NHQ])
    ccol = din("ccol", [128, 8])
    w_ada = din("w_ada", [D, 6 * D])
    b_ada = din("b_ada", [1, 6 * D])
    g_attn = din("g_attn", [128, 8])
    w_in = din("w_in", [D, N_IN])
    bfg = din("bfg", [1, 8])
    g_out = din("g_out", [1, D])
    w_out = din("w_out", [D, D])
    g_mlp = din("g_mlp", [128, 8])
    w_up = din("w_up", [D, 2 * DFF])
    cw = din("cw", [128, 2 * NG, 3])
    cb = din("cb", [128, 2 * NG])
    w_down = din("w_down", [DFF, D])
    g_final = din("g_final", [1, D])
    c_ident = din("c_ident", [128, 128])
    c_triu = din("c_triu", [128, 128])
    c_ones = din("c_ones", [128, 128])
    c_trisb = din("c_trisb", [128, 128])
    c_trif = din("c_trif", [128, 2, 128])
    c_tris = din("c_tris", [128, 2, 128])
    c_onehot = din("c_onehot", [8, S])
    c_qmask = din("c_qmask", [8, SOH])
    c_ones3 = din("c_ones3", [3, SOH])
    h_mbias = din("h_mbias", [128, NB, NHQ])
    h_m01 = din("h_m01", [128, NB, NHQ])
    c_zero = din("c_zero", [128, D])
    out = nc.dram_tensor("out", [SO, D], F32, kind="ExternalOutput").ap()

    mod_d = dscr("mod_d", [1, 6 * D], F32)
    Qd = dscr("Qd", [NH, HD, SOH], BF16)
    Kd = dscr("Kd", [NH, HD, S], BF16)
    Vd = dscr("Vd", [NH, 128, NB, HD], BF16)
    RVd = dscr("RVd", [8, SOH], BF16)
    KFd = dscr("KFd", [8, 3, S], BF16)
    Fd = dscr("Fd", [128, 8, NB], F32)
    Rd = dscr("Rd", [128, NSL + 1, 8], F32)
    MIXd = dscr("MIXd", [SOH, D], BF16)
    X1d = dscr("X1d", [SOH, D], F32)
    bd = {n: Buf(n) for n in ['mod_d', 'Qd', 'Kd', 'Vd', 'RVd', 'Fd', 'Rd', 'MIXd', 'X1d', 'out']}

    P = Prog(nc)

    def col_of(off):
        return mod_d[0, off:off + D].rearrange("(c p) -> p c", p=128)

    with contextlib.ExitStack() as es:
        C = Ctx(nc, es)
        cc, b_cc = C.sb([128, 8], F32)
        sc, b_sc = C.sb([128, 8], F32)
        sg0, b_sg0 = C.sb([128, 8], F32)
        wa = [C.sb([128, 8, 512], F32) for _ in range(2)]
        brow, b_brow = C.sb([1, 6 * D], F32)
        mrow, b_mrow = C.sb([1, 6 * D], F32)
        pm = [C.ps(), C.ps()]
        P.op('sp', lambda e: e.dma_start(out=cc[:], in_=ccol), writes=[b_cc], dma=True)
        P.op('sp', lambda e: e.dma_start(out=brow[:], in_=b_ada), writes=[b_brow], dma=True)
        P.op('act', lambda e: e.activation(out=sg0[:], in_=cc[:], func=AF.Sigmoid), reads=[b_cc], writes=[b_sg0])
        P.op('dve', lambda e: e.tensor_tensor(out=sc[:], in0=sg0[:], in1=cc[:], op=ALU.mult),
             reads=[b_sg0, b_cc], writes=[b_sc])
        wav = w_ada.rearrange("(c p) n -> p c n", p=128)
        for n in range(12):
            wt, b_wt = wa[n % 2]
            pt, b_pt = pm[n % 2]
            P.op('sp', lambda e, wt=wt, n=n: e.dma_start(out=wt[:], in_=wav[:, :, n * 512:(n + 1) * 512]),
                 writes=[b_wt], dma=True)
            for kc in range(8):
                P.op('pe', lambda e, pt=pt, wt=wt, kc=kc: e.matmul(pt[0:1, :], lhsT=sc[:, kc:kc + 1], rhs=wt[:, kc, :],
                                                                  start=(kc == 0), stop=(kc == 7)),
                     reads=[b_sc, b_wt], writes=[b_pt])
            P.op('dve', lambda e, pt=pt, n=n: e.tensor_tensor(out=mrow[0:1, n * 512:(n + 1) * 512], in0=pt[0:1, :],
                                                             in1=brow[0:1, n * 512:(n + 1) * 512], op=ALU.add),
                 reads=[b_pt, b_brow], writes=[b_mrow])
        P.op('sp', lambda e: e.dma_start(out=mod_d, in_=mrow[:]), reads=[b_mrow], writes=[bd['mod_d']], dma=True)
        P.run_phase()
        if P.phase >= nph:
            return nc

    with contextlib.ExitStack() as es:
        C = Ctx(nc, es)
        wi, b_wi = C.sb([128, 8, N_IN], BF16)
        idf, b_idf = C.sb([128, 128], F32)
        triu, b_triu = C.sb([128, 128], F32)
        onesf, b_onesf = C.sb([128, 128], F32)
        gcol, b_gcol = C.sb([128, 8], F32)
        scl, b_scl = C.sb([128, 8], F32)
        gsA, b_gsA = C.sb([128, 8], F32)
        shA, b_shA = C.sb([128, 8], F32)
        bf4, b_bf4 = C.sb([128, 4, 8], F32)
        selt, b_selt = C.sb([8, 2], F32)
        xts = [C.sb([128, 4, D], F32) for _ in range(2)]
        xs, b_xs = C.sb([128, 4, D], F32)
        junk, b_junk = C.sb([128, D], BF16)
        ss, b_ss = C.sb([128, 4], F32)
        rstd, b_rstd = C.sb([128, 4], F32)
        hTs = [C.sb([128, 8, 512], BF16) for _ in range(2)]
        qko = [C.sb([128, 512], BF16) for _ in range(4)]
        vsb = [C.sb([128, 4, D], BF16) for _ in range(2)]
        zt, b_zt = C.sb([128, 32], F32)
        et, b_et = C.sb([128, 32], F32)
        spf, b_spf = C.sb([128, 32], F32)
        fblk, b_fblk = C.sb([128, 4, 8], F32)
        carry, b_carry = C.sb([128, 8], F32)
        fneg, b_fneg = C.sb([128, 8, NB], F32)
        rbc, b_rbc = C.sb([128, NSL + 1, 8], F32)
        fte, b_fte = C.sb([8, 512], F32)
        fto, b_fto = C.sb([8, 512], F32)
        fsel, b_fsel = C.sb([8, 512], F32)
        hs, b_hs = C.sb([8, NHQ], F32)
        rv, b_rv = C.sb([8, 512], BF16)
        rvh, b_rvh = C.sb([8, NHQ], BF16)
        kf, b_kf = C.sb([8, 3, 512], BF16)
        kr1, b_kr1 = C.sb([8, 512], F32)
        kr2, b_kr2 = C.sb([8, 512], F32)
        psT = [C.ps() for _ in range(2)]
        psP = [C.ps() for _ in range(3)]
        psF, b_psF = C.ps()
        psC, b_psC = C.ps()
        psFT, b_psFT = C.ps()

        wiv = w_in.rearrange("(c p) n -> p c n", p=128)
        for k4 in range(4):
            lo, hi = k4 * 770, (k4 + 1) * 770
            P.op('pool', lambda e, lo=lo, hi=hi: e.dma_start(out=wi[:, :, lo:hi], in_=wiv[:, :, lo:hi]),
                 writes=[b_wi], dma=True)
        P.op('sp', lambda e: e.dma_start(out=idf[:], in_=c_ident), writes=[b_idf], dma=True)
        P.op('sp', lambda e: e.dma_start(out=triu[:], in_=c_triu), writes=[b_triu], dma=True)
        P.op('sp', lambda e: e.dma_start(out=onesf[:], in_=c_ones), writes=[b_onesf], dma=True)
        P.op('sp', lambda e: e.dma_start(out=gcol[:], in_=g_attn), writes=[b_gcol], dma=True)
        P.op('sp', lambda e: e.dma_start(out=selt[:], in_=selw), writes=[b_selt], dma=True)
        P.op('sp', lambda e: e.dma_start(out=shA[:], in_=col_of(0), allow_slow_non_contiguous=True),
             reads=[bd['mod_d']], writes=[b_shA], dma=True)
        P.op('sp', lambda e: e.dma_start(out=scl[:], in_=col_of(D), allow_slow_non_contiguous=True),
             reads=[bd['mod_d']], writes=[b_scl], dma=True)
        for j in range(4):
            P.op('sp', lambda e, j=j: e.dma_start(out=bf4[:, j, :], in_=bfg[0, :].partition_broadcast(128)),
                 writes=[b_bf4], dma=True)
        P.op('dve', lambda e: e.tensor_scalar(out=scl[:], in0=scl[:], scalar1=1.0, scalar2=None, op0=ALU.add),
             reads=[b_scl], writes=[b_scl])
        P.op('dve', lambda e: e.tensor_tensor(out=gsA[:], in0=scl[:], in1=gcol[:], op=ALU.mult),
             reads=[b_scl, b_gcol], writes=[b_gsA])
        P.op('dve', lambda e: e.memset(carry[:], 0.0), writes=[b_carry])
        P.op('dve', lambda e: e.memset(fto[:], 0.0), writes=[b_fto])

        xv = x.rearrange("(t j p) d -> t p j d", j=4, p=128)
        xov = xo[0:SO, :].rearrange("(t j p) d -> t p j d", j=4, p=128)
        Qv = Qd.rearrange("h d s -> (h d) s")
        Kv = Kd.rearrange("h d s -> (h d) s")
        cn = {'qk': 0, 'pn': 0}

        work = [('full', t) for t in range(NT)] + [('own', i) for i in range(NSL)] + [('halo', 0)]

        def src_of(w):
            kind, idx = w
            if kind == 'full':
                return xv[idx], 4
            if kind == 'own':
                return xov[idx], 4
            return xo[SO:SO + 128, :], 1

        def issue_load(wn):
            srcap, nj = src_of(work[wn])
            xt, b_xt = xts[wn % 2]
            if nj == 4:
                P.op('sp', lambda e: e.dma_start(out=xt[:], in_=srcap), writes=[b_xt], dma=True)
            else:
                P.op('sp', lambda e: e.dma_start(out=xt[:, 0, :], in_=srcap), writes=[b_xt], dma=True)

        issue_load(0)
        for wn, (kind, idx) in enumerate(work):
            nj = 1 if kind == 'halo' else 4
            NTK = 128 * nj
            xt, b_xt = xts[wn % 2]
            hT, b_hT = hTs[wn % 2]
            if wn + 1 < len(work):
                issue_load(wn + 1)
            for j in range(nj):
                P.op('act', lambda e, xt=xt, j=j: e.activation(out=junk[:], in_=xt[:, j, :], func=AF.Square,
                                                              accum_out=ss[:, j:j + 1]),
                     reads=[b_xt], writes=[b_junk, b_ss])
            P.op('act', lambda e, nj=nj: e.activation(out=rstd[:, 0:nj], in_=ss[:, 0:nj], func=AF.Ln, scale=1.0 / D, bias=EPS),
                 reads=[b_ss], writes=[b_rstd])
            P.op('act', lambda e, nj=nj: e.activation(out=rstd[:, 0:nj], in_=rstd[:, 0:nj], func=AF.Exp, scale=-0.5),
                 reads=[b_rstd], writes=[b_rstd])
            for j in range(nj):
                if j % 2 == 0:
                    P.op('dve', lambda e, xt=xt, j=j: e.tensor_scalar(out=xs[:, j, :], in0=xt[:, j, :], scalar1=rstd[:, j:j + 1],
                                                                     scalar2=None, op0=ALU.mult),
                         reads=[b_xt, b_rstd], writes=[b_xs])
                else:
                    P.op('act', lambda e, xt=xt, j=j: e.activation(out=xs[:, j, :], in_=xt[:, j, :], func=AF.Copy,
                                                                  scale=rstd[:, j:j + 1]),
                         reads=[b_xt, b_rstd], writes=[b_xs])
            for c in range(8):
                pt, b_pt = psT[c % 2]
                for j in range(nj):
                    P.op('pe', lambda e, pt=pt, j=j, c=c: e.transpose(out=pt[:, j * 128:(j + 1) * 128],
                                                                     in_=xs[:, j, c * 128:(c + 1) * 128], identity=idf[:]),
                         reads=[b_xs, b_idf], writes=[b_pt])
                if c % 2 == 0:
                    P.op('act', lambda e, pt=pt, hT=hT, c=c, NTK=NTK: e.activation(out=hT[:, c, 0:NTK], in_=pt[:, 0:NTK], func=AF.Identity,
                                                                                  scale=gsA[:, c:c + 1], bias=shA[:, c:c + 1]),
                         reads=[b_pt, b_gsA, b_shA], writes=[b_hT])
                else:
                    P.op('dve', lambda e, pt=pt, hT=hT, c=c, NTK=NTK: e.tensor_scalar(out=hT[:, c, 0:NTK], in0=pt[:, 0:NTK], scalar1=gsA[:, c:c + 1],
                                                                                     scalar2=shA[:, c:c + 1], op0=ALU.mult, op1=ALU.add),
                         reads=[b_pt, b_gsA, b_shA], writes=[b_hT])
            for Pp in range(8):
                if kind == 'full':
                    col0 = 512 + 128 * Pp if Pp < 4 else 2048 + 128 * (Pp - 4)
                else:
                    col0 = 128 * Pp if Pp < 4 else 1536 + 128 * (Pp - 4)
                pp, b_pp = psP[cn['pn'] % 3]
                cn['pn'] += 1
                ot, b_ot = qko[cn['qk'] % 4]
                cn['qk'] += 1
                for c in range(8):
                    P.op('pe', lambda e, pp=pp, hT=hT, c=c, col0=col0, NTK=NTK: e.matmul(pp[:, 0:NTK], lhsT=wi[:, c, col0:col0 + 128],
                                                                                        rhs=hT[:, c, 0:NTK], start=(c == 0), stop=(c == 7)),
                         reads=[b_wi, b_hT], writes=[b_pp])
                if kind == 'full':
                    P.op('dve', lambda e, pp=pp, ot=ot: e.tensor_copy(out=ot[:], in_=pp[:]), reads=[b_pp], writes=[b_ot])
                    P.op('sp', lambda e, ot=ot, Pp=Pp, idx=idx: e.dma_start(out=Kv[Pp * 128:(Pp + 1) * 128, idx * 512:(idx + 1) * 512], in_=ot[:]),
                         reads=[b_ot], dma=True)
                else:
                    q0 = idx * 512 if kind == 'own' else SO
                    P.op('act', lambda e, pp=pp, ot=ot, NTK=NTK: e.activation(out=ot[:, 0:NTK], in_=pp[:, 0:NTK], func=AF.Copy, scale=0.125),
                         reads=[b_pp], writes=[b_ot])
                    P.op('sp', lambda e, ot=ot, Pp=Pp, q0=q0, NTK=NTK: e.dma_start(out=Qv[Pp * 128:(Pp + 1) * 128, q0:q0 + NTK], in_=ot[:, 0:NTK]),
                         reads=[b_ot], dma=True)
            if kind != 'full':
                continue
            t = idx
            vt, b_vt = vsb[t % 2]
            for j in range(4):
                for half in range(2):
                    vcol = 1024 if half == 0 else 2560
                    pp, b_pp = psP[cn['pn'] % 3]
                    cn['pn'] += 1
                    for c in range(8):
                        P.op('pe', lambda e, pp=pp, hT=hT, c=c, j=j, vcol=vcol: e.matmul(pp[:], lhsT=hT[:, c, j * 128:(j + 1) * 128],
                                                                                        rhs=wi[:, c, vcol:vcol + 512],
                                                                                        start=(c == 0), stop=(c == 7)),
                             reads=[b_wi, b_hT], writes=[b_pp])
                    if (j + half) % 2 == 0:
                        P.op('act', lambda e, pp=pp, vt=vt, j=j, half=half: e.activation(out=vt[:, j, half * 512:(half + 1) * 512],
                                                                                        in_=pp[:], func=AF.Copy),
                             reads=[b_pp], writes=[b_vt])
                    else:
                        P.op('dve', lambda e, pp=pp, vt=vt, j=j, half=half: e.tensor_copy(out=vt[:, j, half * 512:(half + 1) * 512],
                                                                                         in_=pp[:]),
                             reads=[b_pp], writes=[b_vt])
            for H in range(NH):
                P.op('sp', lambda e, vt=vt, H=H, t=t: e.dma_start(out=Vd[H][:, 4 * t:4 * t + 4, :], in_=vt[:, :, HD * H:HD * (H + 1)]),
                     reads=[b_vt], dma=True)
            for j in range(4):
                for c in range(8):
                    P.op('pe', lambda e, hT=hT, c=c, j=j: e.matmul(psF[:, j * 8:(j + 1) * 8], lhsT=hT[:, c, j * 128:(j + 1) * 128],
                                                                  rhs=wi[:, c, 3072:3080], start=(c == 0), stop=(c == 7)),
                         reads=[b_wi, b_hT], writes=[b_psF])
            P.op('dve', lambda e: e.tensor_tensor(out=zt[:], in0=psF[:, 0:32], in1=bf4[:].rearrange("p a b -> p (a b)"), op=ALU.add),
                 reads=[b_psF, b_bf4], writes=[b_zt])
            P.op('act', lambda e: e.activation(out=et[:], in_=zt[:], func=AF.Exp, scale=-1.0), reads=[b_zt], writes=[b_et])
            P.op('act', lambda e: e.activation(out=spf[:], in_=et[:], func=AF.Ln, bias=1.0), reads=[b_et], writes=[b_spf])
            for j in range(4):
                P.op('pe', lambda e, j=j: e.matmul(psC[:, 0:8], lhsT=triu[:], rhs=spf[:, j * 8:(j + 1) * 8], start=True, stop=True),
                     reads=[b_triu, b_spf], writes=[b_psC])
                P.op('pe', lambda e, j=j: e.matmul(psC[:, 8:16], lhsT=onesf[:], rhs=spf[:, j * 8:(j + 1) * 8], start=True, stop=True),
                     reads=[b_onesf, b_spf], writes=[b_psC])
                P.op('dve', lambda e, j=j: e.tensor_tensor(out=fblk[:, j, :], in0=psC[:, 0:8], in1=carry[:], op=ALU.add),
                     reads=[b_psC, b_carry], writes=[b_fblk])
                P.op('dve', lambda e: e.tensor_tensor(out=carry[:], in0=psC[:, 8:16], in1=carry[:], op=ALU.add),
                     reads=[b_psC, b_carry], writes=[b_carry])
                P.op('pool', lambda e, j=j, t=t: e.tensor_copy(out=fneg[:, :, 4 * t + j], in_=fblk[:, j, :]),
                     reads=[b_fblk], writes=[b_fneg])
            for j in range(4):
                P.op('pe', lambda e, j=j: e.transpose(out=psFT[0:8, j * 128:(j + 1) * 128], in_=fblk[:, j, :], identity=idf[:]),
                     reads=[b_fblk, b_idf], writes=[b_psFT])
            i = t // 2
            fcur, b_fcur = (fte, b_fte) if t % 2 == 0 else (fto, b_fto)

            def emit_split(fcur=fcur, b_fcur=b_fcur, t=t):
                P.op('dve', lambda e: e.tensor_copy(out=kf[:, 0, :], in_=fcur[:]), reads=[b_fcur], writes=[b_kf])
                P.op('dve', lambda e: e.tensor_tensor(out=kr1[:], in0=fcur[:], in1=kf[:, 0, :], op=ALU.subtract),
                     reads=[b_fcur, b_kf], writes=[b_kr1])
                P.op('dve', lambda e: e.tensor_copy(out=kf[:, 1, :], in_=kr1[:]), reads=[b_kr1], writes=[b_kf])
                P.op('dve', lambda e: e.tensor_tensor(out=kr2[:], in0=kr1[:], in1=kf[:, 1, :], op=ALU.subtract),
                     reads=[b_kr1, b_kf], writes=[b_kr2])
                P.op('dve', lambda e: e.tensor_copy(out=kf[:, 2, :], in_=kr2[:]), reads=[b_kr2], writes=[b_kf])
                P.op('sp', lambda e: e.dma_start(out=KFd[:, :, t * 512:(t + 1) * 512], in_=kf[:]), reads=[b_kf], dma=True)
            if t % 2 == 0:
                P.op('act', lambda e: e.activation(out=fte[:], in_=psFT[0:8, :], func=AF.Copy), reads=[b_psFT], writes=[b_fte])
                emit_split()
                P.op('dve', lambda e, i=i: e.tensor_scalar(out=hs[:, 2 * i:2 * i + 2], in0=fto[:, 510:512], scalar1=selt[:, 0:1],
                                                          scalar2=None, op0=ALU.mult), reads=[b_fto, b_selt], writes=[b_hs])
                P.op('dve', lambda e, i=i: e.scalar_tensor_tensor(out=hs[:, 2 * i:2 * i + 2], in0=fte[:, 510:512], scalar=selt[:, 1:2],
                                                                 in1=hs[:, 2 * i:2 * i + 2], op0=ALU.mult, op1=ALU.add),
                     reads=[b_fte, b_selt, b_hs], writes=[b_hs])
            else:
                P.op('act', lambda e: e.activation(out=fto[:], in_=psFT[0:8, :], func=AF.Copy), reads=[b_psFT], writes=[b_fto])
                emit_split()
                P.op('pool', lambda e, i=i: e.tensor_copy(out=rbc[:, i, :], in_=carry[:]), reads=[b_carry], writes=[b_rbc])
                P.op('dve', lambda e: e.tensor_scalar(out=fsel[:], in0=fte[:], scalar1=selt[:, 0:1], scalar2=None, op0=ALU.mult),
                     reads=[b_fte, b_selt], writes=[b_fsel])
                P.op('dve', lambda e: e.scalar_tensor_tensor(out=fsel[:], in0=fto[:], scalar=selt[:, 1:2], in1=fsel[:],
                                                             op0=ALU.mult, op1=ALU.add), reads=[b_fto, b_selt, b_fsel], writes=[b_fsel])
                P.op('dve', lambda e: e.tensor_scalar(out=rv[:], in0=fsel[:], scalar1=-1.0, scalar2=None, op0=ALU.mult),
                     reads=[b_fsel], writes=[b_rv])
                P.op('sp', lambda e, i=i: e.dma_start(out=RVd[:, i * 512:(i + 1) * 512], in_=rv[:]), reads=[b_rv], dma=True)
                if t == NT - 1:
                    P.op('pool', lambda e: e.tensor_copy(out=rbc[:, NSL, :], in_=carry[:]), reads=[b_carry], writes=[b_rbc])
                    P.op('dve', lambda e: e.tensor_scalar(out=rvh[:], in0=hs[:], scalar1=-1.0, scalar2=None, op0=ALU.mult),
                         reads=[b_hs], writes=[b_rvh])
                    P.op('sp', lambda e: e.dma_start(out=RVd[:, SO:SO + NHQ], in_=rvh[:]), reads=[b_rvh], dma=True)
                    P.op('sp', lambda e: e.dma_start(out=Fd, in_=fneg[:]), reads=[b_fneg], writes=[bd['Fd']], dma=True)
                    P.op('sp', lambda e: e.dma_start(out=Rd, in_=rbc[:]), reads=[b_rbc], writes=[bd['Rd']], dma=True)
        P.run_phase()
        if P.phase >= nph:
            return nc

    with contextlib.ExitStack() as es:
        C = Ctx(nc, es)
        kaug = [C.sb([76, S], BF16) for _ in range(4)]
        qaug = [C.sb([76, SOH], BF16) for _ in range(4)]
        vts = [C.sb([128, NB, 65], BF16) for _ in range(4)]
        trisb, b_trisb = C.sb([128, 128], BF16)
        idb, b_idb = C.sb([128, 128], BF16)
        trif, b_trif = C.sb([128, 2, 128], BF16)
        tris, b_tris = C.sb([128, 2, 128], BF16)
        hmb, b_hmb = C.sb([128, NB, NHQ], BF16)
        hm01, b_hm01 = C.sb([128, NB, NHQ], BF16)
        onec, b_onec = C.sb([128, 1], BF16)
        gbc, b_gbc = C.sb([128, D], F32)
        fneg, b_fneg = C.sb([128, 8, NB], F32)
        rbc, b_rbc = C.sb([128, NSL + 1, 8], F32)
        zer, b_zer = C.sb([128, 256], BF16)
        biases = [C.sb([128, NB], F32) for _ in range(2)]
        pts = [C.sb([128, 512], BF16) for _ in range(3)]
        ats = [C.sb([128, 512], BF16) for _ in range(3)]
        Es = [C.sb([128, 512], F32) for _ in range(3)]
        sps = [C.sb([128, 512], BF16) for _ in range(3)]
        eCs = [C.sb([128, 512], F32) for _ in range(2)]
        carrs = [C.sb([128, 4], F32) for _ in range(2)]
        ecar = [C.sb([128, 4], F32) for _ in range(4)]
        oaccs = [C.sb([128, 4, HD], F32) for _ in range(2)]
        osbs = [C.sb([128, 4, HD], F32) for _ in range(2)]
        rtmp, b_rtmp = C.sb([128, 4, HD], F32)
        rden, b_rden = C.sb([128, 4], F32)
        ssq, b_ssq = C.sb([128, 4], F32)
        rn, b_rn = C.sb([128, 4], F32)
        junk2, b_junk2 = C.sb([128, HD], F32)
        ys = [C.sb([128, 4, HD], BF16) for _ in range(2)]
        ysF = [C.sb([128, 4, HD], BF16) for _ in range(2)]
        psS = [C.ps() for _ in range(3)]
        psCb = [C.ps() for _ in range(2)]
        psO = [C.ps() for _ in range(2)]
        psFo = C.ps()

        P.op('pool', lambda e: e.dma_start(out=trisb[:], in_=c_trisb), writes=[b_trisb], dma=True)
        P.op('pool', lambda e: e.dma_start(out=idb[:], in_=c_ident), writes=[b_idb], dma=True)
        P.op('pool', lambda e: e.dma_start(out=trif[:], in_=c_trif), writes=[b_trif], dma=True)
        P.op('pool', lambda e: e.dma_start(out=tris[:], in_=c_tris), writes=[b_tris], dma=True)
        P.op('pool', lambda e: e.dma_start(out=hmb[:], in_=h_mbias), writes=[b_hmb], dma=True)
        P.op('pool', lambda e: e.dma_start(out=hm01[:], in_=h_m01), writes=[b_hm01], dma=True)
        P.op('pool', lambda e: e.dma_start(out=zer[:], in_=c_zero[:, 0:256]), writes=[b_zer], dma=True)
        for z4 in range(4):
            P.op('sp', lambda e, z4=z4: e.dma_start(out=MIXd[SO + NHQ:SOH, 256 * z4:256 * (z4 + 1)], in_=zer[0:128 - NHQ, :]),
                 reads=[b_zer], dma=True)
        P.op('sp', lambda e: e.dma_start(out=gbc[:], in_=g_out[0, :].partition_broadcast(128)), writes=[b_gbc], dma=True)
        P.op('dve', lambda e: e.memset(onec[:], 1.0), writes=[b_onec])
        for i2 in range(2):
            ka, b_ka = kaug[i2]
            vt, b_vt = vts[i2]
            P.op('dve', lambda e, ka=ka: e.memset(ka[64:65, :], 1.0), writes=[b_ka])
            P.op('pool', lambda e, vt=vt: e.memset(vt[:, :, 64:65], 1.0), writes=[b_vt])
        for i4 in range(4):
            ka, b_ka = kaug[i4]
            qa, b_qa = qaug[i4]
            rb = 68 if i4 < 2 else 64
            if i4 < 2:
                P.op('pool', lambda e, qa=qa: e.dma_start(out=qa[65:68, :], in_=c_ones3), writes=[b_qa], dma=True)
            P.op('pool', lambda e, ka=ka, rb=rb: e.dma_start(out=ka[rb:rb + 8, :], in_=c_onehot), writes=[b_ka], dma=True)
            P.op('pool', lambda e, qa=qa, rb=rb: e.dma_start(out=qa[rb:rb + 8, :], in_=c_qmask), writes=[b_qa], dma=True)

        def hset(Hn):
            return (Hn % 2) if Hn < 8 else 2 + (Hn % 2)

        MIXv = MIXd[0:SO, :].rearrange("(t m p) c -> t p m c", m=4, p=128)

        def load_head(Hn):
            ka, b_ka = kaug[hset(Hn)]
            qa, b_qa = qaug[hset(Hn)]
            vt, b_vt = vts[hset(Hn)]
            P.op('sp', lambda e: e.dma_start(out=ka[0:64, :], in_=Kd[Hn]), reads=[bd['Kd']], writes=[b_ka], dma=True)
            P.op('sp', lambda e: e.dma_start(out=qa[0:64, :], in_=Qd[Hn]), reads=[bd['Qd']], writes=[b_qa], dma=True)
            if Hn < 8:
                P.op('sp', lambda e: e.dma_start(out=qa[64:65, :], in_=RVd[Hn:Hn + 1, :]),
                     reads=[bd['RVd']], writes=[b_qa], dma=True)
                P.op('sp', lambda e: e.dma_start(out=ka[65:68, :], in_=KFd[Hn]), writes=[b_ka], dma=True)
            P.op('sp', lambda e: e.dma_start(out=vt[:, :, 0:64], in_=Vd[Hn]), reads=[bd['Vd']], writes=[b_vt], dma=True)

        def slot_desc(sl):
            if sl < NSL:
                return dict(q0=512 * sl, NQ=512, MB=4, QP=128, nkb=8 * sl + 8, km=8 * sl, halo=False, sl=sl)
            return dict(q0=SO, NQ=NHQ, MB=1, QP=NHQ, nkb=NB, km=0, halo=True, sl=sl)

        def epilogue(src, b_src, yt, b_yt, H, sd):
            MB, QP = sd['MB'], sd['QP']
            for m in range(MB):
                P.op('act', lambda e, m=m: e.activation(out=junk2[0:QP, :], in_=src[0:QP, m, :], func=AF.Square,
                                                        accum_out=ssq[0:QP, m:m + 1]),
                     reads=[b_src], writes=[b_junk2, b_ssq])
            P.op('act', lambda e: e.activation(out=rn[0:QP, 0:MB], in_=ssq[0:QP, 0:MB], func=AF.Ln, scale=1.0 / HD, bias=EPS),
                 reads=[b_ssq], writes=[b_rn])
            P.op('act', lambda e: e.activation(out=rn[0:QP, 0:MB], in_=rn[0:QP, 0:MB], func=AF.Exp, scale=-0.5),
                 reads=[b_rn], writes=[b_rn])
            for m in range(MB):
                P.op('dve', lambda e, m=m: e.scalar_tensor_tensor(
                    out=yt[0:QP, m, :], in0=src[0:QP, m, :], scalar=rn[0:QP, m:m + 1], in1=gbc[0:QP, HD * H:HD * (H + 1)],
                    op0=ALU.mult, op1=ALU.mult), reads=[b_src, b_rn, b_gbc], writes=[b_yt])
            if sd['halo']:
                P.op('sp', lambda e: e.dma_start(out=MIXd[SO:SO + NHQ, HD * H:HD * (H + 1)], in_=yt[0:NHQ, 0, :]),
                     reads=[b_yt], dma=True)
            else:
                P.op('sp', lambda e: e.dma_start(out=MIXv[sd['sl']][:, :, HD * H:HD * (H + 1)], in_=yt[:]),
                     reads=[b_yt], dma=True)

        iters = []
        defer = []
        defer_epi = []
        tails = []
        step_now = [0]
        cnt = {'S': 0, 'P': 0, 'tile': 0, 'sbn': 0, 'ftile': 0, 'stile': 0}

        def fox_iter(ftl, pos, H, sd, kb, first_head_iter, ka, b_ka, qa, b_qa, vt, b_vt, bi, b_bi, po, b_po, yt, b_yt):
            pS, b_pS = psS[pos % 3]
            pt, b_pt = pts[cnt['P'] % 3]
            cnt['P'] += 1
            pov = po[:].rearrange("p (m c) -> p m c", c=128)
            q0, NQ, MB, QP, nkb = sd['q0'], sd['NQ'], sd['MB'], sd['QP'], sd['nkb']
            masked = kb >= sd['km']
            sl = sd['sl']

            def st0():
                KR = 76 if (masked and not sd['halo']) else 68
                P.op('pe', lambda e: e.matmul(pS[:, 0:NQ], lhsT=ka[0:KR, kb * 128:(kb + 1) * 128], rhs=qa[0:KR, q0:q0 + NQ],
                                              start=True, stop=not masked), reads=[b_ka, b_qa], writes=[b_pS])
                if masked:
                    if sd['halo']:
                        P.op('pe', lambda e: e.matmul(pS[:, 0:NQ], lhsT=idb[:], rhs=hmb[:, kb, :], start=False, stop=True),
                             reads=[b_idb, b_hmb], writes=[b_pS])
                    else:
                        jj = kb - sd['km']
                        mc = 128 * (jj % 4)
                        P.op('pe', lambda e: e.matmul(pS[:, mc:mc + 128], lhsT=idb[:], rhs=trif[:, jj // 4, :], start=False, stop=True),
                             reads=[b_idb, b_trif], writes=[b_pS])

            def st1():
                P.op('act', lambda e: e.activation(out=pt[:, 0:NQ], in_=pS[:, 0:NQ], func=AF.Exp),
                     reads=[b_pS], writes=[b_pt])

            def st2():
                for m in range(MB):
                    P.op('pe', lambda e, m=m: e.matmul(pov[0:QP, m, 0:65], lhsT=pt[:, m * 128:m * 128 + QP], rhs=vt[:, kb, 0:65],
                                                       start=(kb == 0 and m == 0), stop=(kb == nkb - 1)),
                         reads=[b_pt, b_vt], writes=[b_po])
                if first_head_iter and H + 1 < 8:
                    defer.append((step_now[0] + 5, H + 1))
                    defer.append((step_now[0] + 5, H + 9))
                if kb == nkb - 1:
                    osb, b_osb = osbs[ftl % 2]
                    P.op('dve', lambda e: e.reciprocal(out=rden[0:QP, 0:MB], in_=pov[0:QP, 0:MB, 64]), reads=[b_po], writes=[b_rden])
                    for m in range(MB):
                        P.op('dve', lambda e, m=m: e.tensor_scalar(out=osb[0:QP, m, :], in0=pov[0:QP, m, 0:64],
                                                                   scalar1=rden[0:QP, m:m + 1], scalar2=None, op0=ALU.mult),
                             reads=[b_po, b_rden], writes=[b_osb])
                    defer_epi.append((step_now[0] + 3, (osb, b_osb, yt, b_yt, H, sd)))
            return [st0, st1, st2]

        def sb_iter(pos, H, sd, kb, first, last, first_head_iter, ka, b_ka, qa, b_qa, vt, b_vt, carr, b_carr, oacc, b_oacc, yt, b_yt):
            pS, b_pS = psS[pos % 3]
            n = cnt['sbn']
            cnt['sbn'] += 1
            pC, b_pC = psCb[n % 2]
            pv, b_pv = psO[n % 2]
            pcs, b_pcs = pv[:, 64:68], b_pv
            pvv = pv[:].rearrange("p (m c) -> p m c", c=128)
            Et, b_Et = Es[n % 3]
            st, b_st = sps[n % 3]
            eC, b_eC = eCs[n % 2]
            ec, b_ec = ecar[n % 4]
            at, b_at = ats[n % 3]
            q0, NQ, MB, QP, nkb = sd['q0'], sd['NQ'], sd['MB'], sd['QP'], sd['nkb']
            masked = kb >= sd['km']

            def s0():
                KR = 72 if (masked and not sd['halo']) else 64
                P.op('pe', lambda e: e.matmul(pS[:, 0:NQ], lhsT=ka[0:KR, kb * 128:(kb + 1) * 128], rhs=qa[0:KR, q0:q0 + NQ],
                                              start=True, stop=not masked), reads=[b_ka, b_qa], writes=[b_pS])
                if masked:
                    if sd['halo']:
                        P.op('pe', lambda e: e.matmul(pS[:, 0:NQ], lhsT=idb[:], rhs=hm01[:, kb, :], start=False, stop=True),
                             reads=[b_idb, b_hm01], writes=[b_pS])
                    else:
                        jj = kb - sd['km']
                        mc = 128 * (jj % 4)
                        P.op('pe', lambda e: e.matmul(pS[:, mc:mc + 128], lhsT=idb[:], rhs=tris[:, jj // 4, :], start=False, stop=True),
                             reads=[b_idb, b_tris], writes=[b_pS])

            def s1():
                P.op('act', lambda e: e.activation(out=Et[:, 0:NQ], in_=pS[:, 0:NQ], func=AF.Exp), reads=[b_pS], writes=[b_Et])
                tails.append(lambda: P.op('act', lambda e: e.activation(out=st[:, 0:NQ], in_=Et[:, 0:NQ], func=AF.Ln, bias=1.0),
                                          reads=[b_Et], writes=[b_st]))

            def s2():
                P.op('pe', lambda e: e.matmul(pC[:, 0:NQ], lhsT=trisb[:], rhs=st[:, 0:NQ], start=True, stop=True),
                     reads=[b_trisb, b_st], writes=[b_pC])
                for m in range(MB):
                    P.op('pe', lambda e, m=m: e.matmul(pcs[0:QP, m:m + 1], lhsT=st[:, m * 128:m * 128 + QP], rhs=onec[:],
                                                       start=True, stop=True), reads=[b_st, b_onec], writes=[b_pcs])

            def s3():
                if first:
                    P.op('pool', lambda e: e.memset(carr[:], 0.0), writes=[b_carr])
                    P.op('pool', lambda e: e.memset(oacc[:], 0.0), writes=[b_oacc])
                P.op('act', lambda e: e.activation(out=eC[:, 0:NQ], in_=pC[:, 0:NQ], func=AF.Exp, scale=-1.0), reads=[b_pC], writes=[b_eC])
                P.op('act', lambda e: e.activation(out=ec[0:QP, 0:MB], in_=carr[0:QP, 0:MB], func=AF.Exp, scale=-1.0),
                     reads=[b_carr], writes=[b_ec])
                P.op('dve', lambda e: e.tensor_tensor(out=at[:, 0:NQ], in0=Et[:, 0:NQ], in1=eC[:, 0:NQ], op=ALU.mult),
                     reads=[b_Et, b_eC], writes=[b_at])
                P.op('dve', lambda e: e.tensor_tensor(out=carr[0:QP, 0:MB], in0=pcs[0:QP, 0:MB], in1=carr[0:QP, 0:MB], op=ALU.add),
                     reads=[b_pcs, b_carr], writes=[b_carr])

            def s4():
                for m in range(MB):
                    P.op('pe', lambda e, m=m: e.matmul(pvv[0:QP, m, 0:64], lhsT=at[:, m * 128:m * 128 + QP], rhs=vt[:, kb, 0:64],
                                                       start=True, stop=True), reads=[b_at, b_vt], writes=[b_pv])

            def s5():
                if MB == 4:
                    P.op('dve', lambda e: e.tensor_tensor(out=rtmp[:], in0=pvv[:, :, 0:64],
                                                          in1=ec[:, 0:4].unsqueeze(2).to_broadcast([128, 4, HD]), op=ALU.mult),
                         reads=[b_pv, b_ec], writes=[b_rtmp])
                    P.op('dve', lambda e: e.tensor_tensor(out=oacc[:], in0=oacc[:], in1=rtmp[:], op=ALU.add),
                         reads=[b_rtmp, b_oacc], writes=[b_oacc])
                else:
                    for m in range(MB):
                        P.op('dve', lambda e, m=m: e.scalar_tensor_tensor(
                            out=oacc[0:QP, m, :], in0=pvv[0:QP, m, 0:64], scalar=ec[0:QP, m:m + 1], in1=oacc[0:QP, m, :],
                            op0=ALU.mult, op1=ALU.add), reads=[b_pv, b_ec, b_oacc], writes=[b_oacc])
                if last:
                    defer_epi.append((step_now[0] + 3, (oacc, b_oacc, yt, b_yt, H, sd)))
            return [s0, s1, s2, s3, s4, s5]

        load_head(0)
        load_head(8)
        for Hp in range(8):
            lists = []
            for H in (Hp, 8 + Hp):
                fox = H < 8
                ka, b_ka = kaug[hset(H)]
                qa, b_qa = qaug[hset(H)]
                vt, b_vt = vts[hset(H)]
                fh = fox
                li = []
                for sl in range(NSL + 1):
                    sd = slot_desc(sl)
                    nkb = sd['nkb']
                    if fox:
                        tl = cnt['ftile']
                        cnt['ftile'] += 1
                        yt, b_yt = ysF[tl % 2]
                        bi, b_bi = biases[tl % 2]
                        po, b_po = psFo
                        for kb in range(nkb):
                            li.append(fox_iter(tl, cnt['S'] + 2 * len(li), H, sd, kb, fh, ka, b_ka, qa, b_qa, vt, b_vt, bi, b_bi, po, b_po, yt, b_yt))
                            fh = False
                    else:
                        tl = cnt['stile']
                        cnt['stile'] += 1
                        yt, b_yt = ys[tl % 2]
                        carr, b_carr = carrs[tl % 2]
                        oacc, b_oacc = oaccs[tl % 2]
                        order = list(reversed(range(nkb)))
                        for idx, kb in enumerate(order):
                            li.append(sb_iter(cnt['S'] + 2 * len(li) + 1, H, sd, kb, idx == 0, idx == nkb - 1, False, ka, b_ka, qa, b_qa, vt, b_vt,
                                              carr, b_carr, oacc, b_oacc, yt, b_yt))
                lists.append(li)
            assert len(lists[0]) == len(lists[1])
            cnt['S'] += 2 * len(lists[0])
            for fa, sa in zip(lists[0], lists[1]):
                iters.append(fa)
                iters.append(sa)
        nsteps = len(iters) + 12
        for step in range(nsteps):
            step_now[0] = step
            for (ds, Hn) in list(defer):
                if ds <= step:
                    defer.remove((ds, Hn))
                    load_head(Hn)
            for item in list(defer_epi):
                if item[0] <= step:
                    defer_epi.remove(item)
                    epilogue(*item[1])
            for j in range(6):
                n = step - j
                if 0 <= n < len(iters) and j < len(iters[n]):
                    iters[n][j]()
            for tfn in tails:
                tfn()
            tails.clear()
        P.run_phase()
        if P.phase >= nph:
            return nc

    es_w = contextlib.ExitStack()
    Cw = Ctx(nc, es_w)
    wu, b_wu = Cw.sb([128, 8, 2 * DFF], BF16)
    wd, b_wd = Cw.sb([128, NG, D], BF16)

    with contextlib.ExitStack() as es:
        C = Ctx(nc, es)
        wo, b_wo = C.sb([128, 8, D], BF16)
        idb, b_idb = C.sb([128, 128], BF16)
        gA, b_gA = C.sb([128, D], F32)
        xts = [C.sb([128, 4, D], F32) for _ in range(1)]
        mxs = [C.sb([128, 4, D], BF16) for _ in range(2)]
        mT, b_mT = C.sb([128, 8, 512], BF16)
        tmps = [C.sb([128, 512], F32) for _ in range(2)]
        psT = [C.ps() for _ in range(2)]
        psP = [C.ps() for _ in range(3)]
        P.op('pool', lambda e: e.dma_start(out=wo[:], in_=w_out.rearrange("(c p) n -> p c n", p=128)), writes=[b_wo], dma=True)
        P.op('pool', lambda e: e.dma_start(out=idb[:], in_=c_ident), writes=[b_idb], dma=True)
        wuv = w_up.rearrange("(c p) n -> p c n", p=128)
        for k4 in range(4):
            lo, hi = k4 * 1376, (k4 + 1) * 1376
            P.op('pool', lambda e, lo=lo, hi=hi: e.dma_start(out=wu[:, :, lo:hi], in_=wuv[:, :, lo:hi]), writes=[b_wu], dma=True)
        P.op('pool', lambda e: e.dma_start(out=wd[:, 0:21, :], in_=w_down[0:2688, :].rearrange("(g p) n -> p g n", p=128)),
             writes=[b_wd], dma=True)
        P.op('pool', lambda e: e.dma_start(out=wd[0:64, 21, :], in_=w_down[2688:2752, :]), writes=[b_wd], dma=True)
        P.op('sp', lambda e: e.dma_start(out=gA[:], in_=mod_d[0, 2 * D:3 * D].partition_broadcast(128)),
             reads=[bd['mod_d']], writes=[b_gA], dma=True)
        work = [(i * 512, 4) for i in range(NSL)] + [(SO, 1)]

        def load3a(wn):
            r0, nj = work[wn]
            xt, b_xt = xts[0]
            mx, b_mx = mxs[wn % 2]
            xsrc = xo[r0:r0 + 128 * nj, :].rearrange("(j p) d -> p j d", p=128)
            msrc = MIXd[r0:r0 + 128 * nj, :].rearrange("(j p) d -> p j d", p=128)
            P.op('sp', lambda e: e.dma_start(out=xt[:, 0:nj, :], in_=xsrc), writes=[b_xt], dma=True)
            P.op('sp', lambda e: e.dma_start(out=mx[:, 0:nj, :], in_=msrc), reads=[bd['MIXd']], writes=[b_mx], dma=True)

        load3a(0)
        pn = 0
        for wn, (r0, nj) in enumerate(work):
            xt, b_xt = xts[0]
            mx, b_mx = mxs[wn % 2]
            NTK = 128 * nj
            if wn > 0:
                load3a(wn)
            for c in range(8):
                pt, b_pt = psT[c % 2]
                for j in range(nj):
                    P.op('pe', lambda e, pt=pt, mx=mx, j=j, c=c: e.matmul(pt[:, j * 128:(j + 1) * 128],
                                                                         lhsT=mx[:, j, c * 128:(c + 1) * 128], rhs=idb[:],
                                                                         start=True, stop=True),
                         reads=[b_mx, b_idb], writes=[b_pt])
                if c % 2 == 0:
                    P.op('act', lambda e, pt=pt, c=c, NTK=NTK: e.activation(out=mT[:, c, 0:NTK], in_=pt[:, 0:NTK], func=AF.Copy),
                         reads=[b_pt], writes=[b_mT])
                else:
                    P.op('dve', lambda e, pt=pt, c=c, NTK=NTK: e.tensor_copy(out=mT[:, c, 0:NTK], in_=pt[:, 0:NTK]),
                         reads=[b_pt], writes=[b_mT])
            for j in range(nj):
                for half in range(2):
                    pp, b_pp = psP[pn % 3]
                    tm, b_tm = tmps[pn % 2]
                    pn += 1
                    for c in range(8):
                        P.op('pe', lambda e, pp=pp, c=c, j=j, half=half: e.matmul(pp[:], lhsT=mT[:, c, j * 128:(j + 1) * 128],
                                                                                 rhs=wo[:, c, half * 512:(half + 1) * 512],
                                                                                 start=(c == 0), stop=(c == 7)),
                             reads=[b_mT, b_wo], writes=[b_pp])
                    P.op('dve', lambda e, pp=pp, tm=tm, half=half: e.tensor_tensor(out=tm[:], in0=pp[:],
                                                                                  in1=gA[:, half * 512:(half + 1) * 512], op=ALU.mult),
                         reads=[b_pp, b_gA], writes=[b_tm])
                    P.op('dve', lambda e, tm=tm, xt=xt, j=j, half=half: e.tensor_tensor(
                        out=xt[:, j, half * 512:(half + 1) * 512], in0=tm[:], in1=xt[:, j, half * 512:(half + 1) * 512], op=ALU.add),
                        reads=[b_tm, b_xt], writes=[b_xt])
            dst = X1d[r0:r0 + 128 * nj, :].rearrange("(j p) d -> p j d", p=128)
            P.op('sp', lambda e, xt=xt, dst=dst, nj=nj: e.dma_start(out=dst, in_=xt[:, 0:nj, :]), reads=[b_xt], dma=True)
        P.run_phase()
        if P.phase >= nph:
            return nc

    TK = 256
    NT2 = SO // TK
    with contextlib.ExitStack() as es:
        C = Ctx(nc, es)
        idf, b_idf = C.sb([128, 128], F32)
        gcol, b_gcol = C.sb([128, 8], F32)
        scl, b_scl = C.sb([128, 8], F32)
        gsM, b_gsM = C.sb([128, 8], F32)
        shM, b_shM = C.sb([128, 8], F32)
        gM, b_gM = C.sb([128, D], F32)
        gF, b_gF = C.sb([128, D], F32)
        cwt, b_cwt = C.sb([128, 2 * NG, 3], F32)
        cbt, b_cbt = C.sb([128, 2 * NG], F32)
        hvt, b_hvt = C.sb([128, NHQ], F32)
        x1s = [C.sb([128, 2, D], F32) for _ in range(2)]
        xs, b_xs = C.sb([128, 2, D], F32)
        junk, b_junk = C.sb([128, D], BF16)
        ss, b_ss = C.sb([128, 2], F32)
        rstd, b_rstd = C.sb([128, 2], F32)
        ss2, b_ss2 = C.sb([128, 2], F32)
        rstd2, b_rstd2 = C.sb([128, 2], F32)
        hTs3 = [C.sb([128, 8, TK], BF16) for _ in range(2)]
        ues = [C.sb([128, TK + 2], F32) for _ in range(4)]
        halo, b_halo = C.sb([128, 2 * NG, 2], F32)
        uh, b_uh = C.sb([128, 2 * NG, NHQ], F32)
        t1s = [C.sb([128, TK], F32) for _ in range(4)]
        sgt = [C.sb([128, TK], F32) for _ in range(2)]
        aT, b_aT = C.sb([128, NG, TK], BF16)
        tmps = [C.sb([128, 512], F32) for _ in range(2)]
        psT = [C.ps() for _ in range(2)]
        psU = [C.ps() for _ in range(2)]
        psD = [C.ps() for _ in range(4)]

        P.op('sp', lambda e: e.dma_start(out=idf[:], in_=c_ident), writes=[b_idf], dma=True)
        P.op('sp', lambda e: e.dma_start(out=gcol[:], in_=g_mlp), writes=[b_gcol], dma=True)
        P.op('sp', lambda e: e.dma_start(out=hvt[:], in_=hvalid), writes=[b_hvt], dma=True)
        P.op('sp', lambda e: e.dma_start(out=shM[:], in_=col_of(3 * D), allow_slow_non_contiguous=True),
             reads=[bd['mod_d']], writes=[b_shM], dma=True)
        P.op('sp', lambda e: e.dma_start(out=scl[:], in_=col_of(4 * D), allow_slow_non_contiguous=True),
             reads=[bd['mod_d']], writes=[b_scl], dma=True)
        P.op('sp', lambda e: e.dma_start(out=gM[:], in_=mod_d[0, 5 * D:6 * D].partition_broadcast(128)),
             reads=[bd['mod_d']], writes=[b_gM], dma=True)
        P.op('sp', lambda e: e.dma_start(out=gF[:], in_=g_final[0, :].partition_broadcast(128)), writes=[b_gF], dma=True)
        P.op('sp', lambda e: e.dma_start(out=cwt[:], in_=cw), writes=[b_cwt], dma=True)
        P.op('sp', lambda e: e.dma_start(out=cbt[:], in_=cb), writes=[b_cbt], dma=True)
        P.op('dve', lambda e: e.tensor_scalar(out=scl[:], in0=scl[:], scalar1=1.0, scalar2=None, op0=ALU.add),
             reads=[b_scl], writes=[b_scl])
        P.op('dve', lambda e: e.tensor_tensor(out=gsM[:], in0=scl[:], in1=gcol[:], op=ALU.mult),
             reads=[b_scl, b_gcol], writes=[b_gsM])
        b_halo_ch = [Buf(f'halo{ch}') for ch in range(2 * NG)]
        b_uh_ch = [Buf(f'uh{ch}') for ch in range(2 * NG)]
        b_aT_g = [Buf(f'aT{g}') for g in range(NG)]
        P.op('pool', lambda e: e.memset(halo[:], 0.0), writes=b_halo_ch)

        x1v = X1d[0:SO, :].rearrange("(t j p) d -> t p j d", j=2, p=128)
        ov = out.rearrange("(t j p) d -> t p j d", j=2, p=128)
        cn = {'nu': 0, 'nd': 0}

        def norm_hT(x1, b_x1, nj, hT, b_hT):
            NTK = 128 * nj
            for j in range(nj):
                P.op('act', lambda e, j=j: e.activation(out=junk[:], in_=x1[:, j, :], func=AF.Square, accum_out=ss[:, j:j + 1]),
                     reads=[b_x1], writes=[b_junk, b_ss])
            P.op('act', lambda e: e.activation(out=rstd[:, 0:nj], in_=ss[:, 0:nj], func=AF.Ln, scale=1.0 / D, bias=EPS),
                 reads=[b_ss], writes=[b_rstd])
            P.op('act', lambda e: e.activation(out=rstd[:, 0:nj], in_=rstd[:, 0:nj], func=AF.Exp, scale=-0.5),
                 reads=[b_rstd], writes=[b_rstd])
            for j in range(nj):
                P.op('act', lambda e, j=j: e.activation(out=xs[:, j, :], in_=x1[:, j, :], func=AF.Copy, scale=rstd[:, j:j + 1]),
                     reads=[b_x1, b_rstd], writes=[b_xs])
            for c in range(8):
                pt, b_pt = psT[c % 2]
                for j in range(nj):
                    P.op('pe', lambda e, pt=pt, j=j, c=c: e.transpose(out=pt[:, j * 128:(j + 1) * 128],
                                                                     in_=xs[:, j, c * 128:(c + 1) * 128], identity=idf[:]),
                         reads=[b_xs, b_idf], writes=[b_pt])
                P.op('act', lambda e, pt=pt, c=c: e.activation(out=hT[:, c, 0:NTK], in_=pt[:, 0:NTK], func=AF.Identity,
                                                              scale=gsM[:, c:c + 1], bias=shM[:, c:c + 1]),
                     reads=[b_pt, b_gsM, b_shM], writes=[b_hT])

        x1h, b_x1h = x1s[1]
        P.op('sp', lambda e: e.dma_start(out=x1h[:, 0, :], in_=X1d[SO:SO + 128, :]), reads=[bd['X1d']], writes=[b_x1h], dma=True)
        P.op('sp', lambda e: e.dma_start(out=x1s[0][0][:], in_=x1v[0]), reads=[bd['X1d']], writes=[x1s[0][1]], dma=True)
        hT, b_hT = hTs3[1]
        norm_hT(x1h, b_x1h, 1, hT, b_hT)
        for g in range(NG):
            w = 128 if g < 21 else 64
            for half in range(2):
                col0 = half * DFF + 128 * g
                ch = half * NG + g
                pu, b_pu = psU[cn['nu'] % 2]
                cn['nu'] += 1
                for c in range(8):
                    P.op('pe', lambda e, pu=pu, c=c, col0=col0, w=w, hT=hT: e.matmul(pu[0:w, 0:NHQ], lhsT=wu[:, c, col0:col0 + w],
                                                                             rhs=hT[:, c, 0:NHQ], start=(c == 0), stop=(c == 7)),
                         reads=[b_wu, b_hT], writes=[b_pu])
                P.op('dve', lambda e, pu=pu, ch=ch, w=w: e.tensor_tensor(out=uh[0:w, ch, :], in0=pu[0:w, 0:NHQ], in1=hvt[0:w, :], op=ALU.mult),
                     reads=[b_pu, b_hvt], writes=[b_uh_ch[ch]])

        for t in range(NT2):
            x1, b_x1 = x1s[t % 2]
            if t + 1 < NT2:
                P.op('sp', lambda e, t=t: e.dma_start(out=x1s[(t + 1) % 2][0][:], in_=x1v[t + 1]),
                     reads=[bd['X1d']], writes=[x1s[(t + 1) % 2][1]], dma=True)
            hT, b_hT = hTs3[t % 2]
            pend = [None, None]
            dq = []
            if t == 0:
                norm_hT(x1, b_x1, 2, hT, b_hT)
            for g in range(NG):
                w = 128 if g < 21 else 64
                res = []
                for half in range(2):
                    col0 = half * DFF + 128 * g
                    ch = half * NG + g
                    pu, b_pu = psU[cn['nu'] % 2]
                    ue, b_ue = ues[cn['nu'] % 4]
                    t1, b_t1 = t1s[cn['nu'] % 4]
                    cn['nu'] += 1
                    for c in range(8):
                        P.op('pe', lambda e, pu=pu, c=c, col0=col0, w=w, hT=hT: e.matmul(pu[0:w, 0:TK], lhsT=wu[:, c, col0:col0 + w],
                                                                                 rhs=hT[:, c, :], start=(c == 0), stop=(c == 7)),
                             reads=[b_wu, b_hT], writes=[b_pu])
                    if t % 2 == 0:
                        s_ = t // 2
                        P.op('pool', lambda e, ue=ue, ch=ch, w=w, s_=s_: e.tensor_copy(out=ue[0:w, 0:2], in_=uh[0:w, ch, 2 * s_:2 * s_ + 2]),
                             reads=[b_uh_ch[ch]], writes=[b_ue])
                    else:
                        P.op('pool', lambda e, ue=ue, ch=ch, w=w: e.tensor_copy(out=ue[0:w, 0:2], in_=halo[0:w, ch, :]),
                             reads=[b_halo_ch[ch]], writes=[b_ue])
                    P.op('act', lambda e, ue=ue, pu=pu, w=w: e.activation(out=ue[0:w, 2:TK + 2], in_=pu[0:w, 0:TK], func=AF.Copy),
                         reads=[b_pu], writes=[b_ue])
                    if t % 2 == 0:
                        P.op('pool', lambda e, ue=ue, ch=ch, w=w: e.tensor_copy(out=halo[0:w, ch, :], in_=ue[0:w, TK:TK + 2]),
                             reads=[b_ue], writes=[b_halo_ch[ch]])
                    P.op('act', lambda e, ue=ue, t1=t1, ch=ch, w=w: e.activation(
                        out=t1[0:w, :], in_=ue[0:w, 0:TK], func=AF.Identity, scale=cwt[0:w, ch, 0:1], bias=cbt[0:w, ch:ch + 1]),
                         reads=[b_ue, b_cwt, b_cbt], writes=[b_t1])
                    P.op('dve', lambda e, ue=ue, t1=t1, ch=ch, w=w: e.scalar_tensor_tensor(
                        out=t1[0:w, :], in0=ue[0:w, 1:TK + 1], scalar=cwt[0:w, ch, 1:2], in1=t1[0:w, :],
                        op0=ALU.mult, op1=ALU.add), reads=[b_ue, b_cwt, b_t1], writes=[b_t1])
                    P.op('dve', lambda e, ue=ue, t1=t1, ch=ch, w=w: e.scalar_tensor_tensor(
                        out=t1[0:w, :], in0=ue[0:w, 2:TK + 2], scalar=cwt[0:w, ch, 2:3], in1=t1[0:w, :],
                        op0=ALU.mult, op1=ALU.add), reads=[b_ue, b_cwt, b_t1], writes=[b_t1])
                    res.append((t1, b_t1))
                (tg, b_tg), (tv, b_tv) = res
                sg, b_sg = sgt[g % 2]

                def gate(sg=sg, b_sg=b_sg, tg=tg, b_tg=b_tg, tv=tv, b_tv=b_tv, g=g, w=w):
                    P.op('act', lambda e: e.activation(out=sg[0:w, :], in_=tg[0:w, :], func=AF.Silu),
                         reads=[b_tg], writes=[b_sg])
                    P.op('dve', lambda e: e.tensor_tensor(out=aT[0:w, g, :], in0=sg[0:w, :], in1=tv[0:w, :], op=ALU.mult),
                         reads=[b_sg, b_tv], writes=[b_aT_g[g]])
                def down(g=g, w=w):
                    for j in range(2):
                        for half in range(2):
                            pd, b_pd = psD[j * 2 + half]
                            P.op('pe', lambda e, pd=pd, j=j, half=half: e.matmul(
                                pd[:], lhsT=aT[0:w, g, j * 128:(j + 1) * 128], rhs=wd[0:w, g, half * 512:(half + 1) * 512],
                                start=(g == 0), stop=(g == NG - 1)), reads=[b_aT_g[g], b_wd], writes=[b_pd])
                if dq:
                    dq.pop(0)()
                if pend[0] is not None:
                    pend[0]()
                    dq.append(pend[1])
                pend[0] = gate
                pend[1] = down
            pend[0]()
            dq.append(pend[1])
            pend[0] = None
            while dq:
                dq.pop(0)()
            if t + 1 < NT2:
                norm_hT(x1s[(t + 1) % 2][0], x1s[(t + 1) % 2][1], 2, hTs3[(t + 1) % 2][0], hTs3[(t + 1) % 2][1])
            for j in range(2):
                for half in range(2):
                    pd, b_pd = psD[j * 2 + half]
                    tm, b_tm = tmps[cn['nd'] % 2]
                    cn['nd'] += 1
                    P.op('dve', lambda e, pd=pd, tm=tm, half=half: e.tensor_tensor(out=tm[:], in0=pd[:],
                                                                                  in1=gM[:, half * 512:(half + 1) * 512], op=ALU.mult),
                         reads=[b_pd, b_gM], writes=[b_tm])
                    P.op('pool', lambda e, tm=tm, x1=x1, j=j, half=half: e.tensor_tensor(
                        out=x1[:, j, half * 512:(half + 1) * 512], in0=tm[:], in1=x1[:, j, half * 512:(half + 1) * 512], op=ALU.add),
                        reads=[b_tm, b_x1], writes=[b_x1])
            for j in range(2):
                P.op('act', lambda e, x1=x1, j=j: e.activation(out=junk[:], in_=x1[:, j, :], func=AF.Square, accum_out=ss2[:, j:j + 1]),
                     reads=[b_x1], writes=[b_junk, b_ss2])
            P.op('act', lambda e: e.activation(out=rstd2[:], in_=ss2[:], func=AF.Ln, scale=1.0 / D, bias=EPS),
                 reads=[b_ss2], writes=[b_rstd2])
            P.op('act', lambda e: e.activation(out=rstd2[:], in_=rstd2[:], func=AF.Exp, scale=-0.5), reads=[b_rstd2], writes=[b_rstd2])
            for j in range(2):
                P.op('dve', lambda e, x1=x1, j=j: e.scalar_tensor_tensor(out=x1[:, j, :], in0=x1[:, j, :], scalar=rstd2[:, j:j + 1],
                                                                        in1=gF[:], op0=ALU.mult, op1=ALU.mult),
                     reads=[b_x1, b_rstd2, b_gF], writes=[b_x1])
            P.op('sp', lambda e, t=t, x1=x1: e.dma_start(out=ov[t], in_=x1[:]), reads=[b_x1], dma=True)
        P.run_phase()
    es_w.close()
    return nc


def host_consts(p, S):
    NB = S // 128
    NSL = S // 1024
    NHQ = 2 * NSL
    j = np.arange(128)[:, None]
    t = np.arange(128)[None, :]
    c = {}
    c['c_ident'] = np.eye(128, dtype=np.float32)
    c['c_triu'] = (j <= t).astype(np.float32)
    c['c_ones'] = np.ones((128, 128), np.float32)
    c['c_trisb'] = (j >= t).astype(np.float32)
    c['c_zero'] = np.zeros((128, D), np.float32)
    k = np.arange(128)[:, None]
    q = np.arange(512)[None, :]
    SO = NSL * 512
    kc = np.arange(128)[:, None]
    cc = np.arange(128)[None, :]
    trif = np.zeros((128, 2, 128), np.float32)
    tris = np.zeros((128, 2, 128), np.float32)
    trif[:, p, :] = np.where(kc <= cc, 0.0, -32768.0)
    tris[:, p, :] = np.where(kc < cc, 0.0, -32768.0)
    c['c_trif'] = trif
    c['c_tris'] = tris
    oh = np.zeros((8, S), np.float32)
    for r in range(8):
        oh[r, :] = ((np.arange(S) // 128) % 8 == r)
    c['c_onehot'] = oh
    qm = np.zeros((8, SO + 128), np.float32)
    mq = (np.arange(SO) % 512) // 128
    for r in range(8):
        qm[r, :SO] = np.where(r > 4 * p + mq, -32768.0, 0.0)
    c['c_qmask'] = qm
    c['c_ones3'] = np.ones((3, SO + 128), np.float32)
    hmb = np.zeros((128, NB, NHQ), np.float32)
    hm01 = np.zeros((128, NB, NHQ), np.float32)
    hval = np.zeros((128, NHQ), np.float32)
    tk = (np.arange(NB)[None, :] * 128 + np.arange(128)[:, None])
    for col in range(NHQ):
        i, e = divmod(col, 2)
        tq = (2 * i + p) * 512 - 2 + e
        if tq >= 0:
            hmb[:, :, col] = np.where(tk <= tq, 0.0, -32768.0)
            hm01[:, :, col] = np.where(tk < tq, 0.0, -32768.0)
            hval[:, col] = 1.0
        else:
            hmb[:, :, col] = np.where(tk == 0, 0.0, -32768.0)
            hm01[:, :, col] = -32768.0
    c['h_mbias'] = hmb
    c['h_m01'] = hm01
    c['hvalid'] = hval
    sel = np.zeros((8, 2), np.float32)
    sel[:, p] = 1.0
    c['selw'] = sel
    return c


def col128(v):
    return np.ascontiguousarray(np.asarray(v, np.float32).reshape(8, 128).T)


def make_weights(w):
    m = {}
    m['w_ada'] = np.ascontiguousarray(w['w_ada'][0])
    m['b_ada'] = np.ascontiguousarray(w['b_ada'][0].reshape(1, -1))
    m['g_attn'] = col128(w['g_attn'][0])
    m['w_in'] = np.ascontiguousarray(w['w_in'][0])
    m['bfg'] = np.ascontiguousarray(w['b_fgate'][0].reshape(1, 8))
    m['g_out'] = np.ascontiguousarray(np.concatenate([w['g_out_fox'][0], w['g_out_sb'][0]]).reshape(1, -1))
    m['w_out'] = np.ascontiguousarray(w['w_out'][0])
    m['g_mlp'] = col128(w['g_mlp'][0])
    m['w_up'] = np.ascontiguousarray(w['w_up'][0])
    cwf = np.asarray(w['conv_w'][0], np.float32)
    cbf = np.asarray(w['conv_b'][0], np.float32)
    cw = np.zeros((128, 2 * NG, 3), np.float32)
    cb = np.zeros((128, 2 * NG), np.float32)
    for half in range(2):
        for g in range(NG):
            wdt = 128 if g < 21 else 64
            lo = half * DFF + 128 * g
            cw[0:wdt, half * NG + g, :] = cwf[:, lo:lo + wdt].T
            cb[0:wdt, half * NG + g] = cbf[lo:lo + wdt]
    m['cw'] = cw
    m['cb'] = cb
    m['w_down'] = np.ascontiguousarray(w['w_down'][0])
    m['g_final'] = np.ascontiguousarray(np.asarray(w['g_final'], np.float32).reshape(1, -1))
    return m


_NC_CACHE = {}


def run(x, c, w, n_cores=8):
    import os
    x = np.asarray(x, np.float32)
    c = np.asarray(c, np.float32)
    B, S, _ = x.shape
    NSL = S // 1024
    SO = NSL * 512
    if S not in _NC_CACHE:
        _NC_CACHE[S] = build(S, int(os.environ.get("MK_NPH", "99")))
    nc = _NC_CACHE[S]
    wm = make_weights(w)
    consts = [host_consts(p, S) for p in range(2)]
    in_maps = []
    for k in range(n_cores):
        b, p = (k // 2) % B, k % 2
        m = dict(wm)
        m.update(consts[p])
        m['x'] = np.ascontiguousarray(x[b])
        m['ccol'] = col128(c[b])
        xo = np.zeros((SO + 128, D), np.float32)
        for i in range(NSL):
            T = 2 * i + p
            xo[512 * i:512 * (i + 1)] = x[b, 512 * T:512 * (T + 1)]
            if T > 0:
                xo[SO + 2 * i:SO + 2 * i + 2] = x[b, 512 * T - 2:512 * T]
        m['xo'] = xo
        in_maps.append(m)
    res = run_bass_kernel_spmd(nc, in_maps, core_ids=list(range(n_cores)))
    if os.environ.get("MK_DBG"):
        global DBG
        DBG = [{k_: np.asarray(v) for k_, v in r.items()} for r in res.results[:2]]
    outv = np.zeros((B, S, D), np.float32)
    for k in range(n_cores):
        b, p = k // 2, k % 2
        if b >= B:
            continue
        o = np.asarray(res.results[k]["out"], np.float32).reshape(SO, D)
        for i in range(NSL):
            T = 2 * i + p
            outv[b, 512 * T:512 * (T + 1)] = o[512 * i:512 * (i + 1)]
    return outv


def kernel(x, c, w_ada, b_ada, g_attn, w_in, b_fgate, g_out_fox, g_out_sb, w_out,
           g_mlp, w_up, conv_w, conv_b, w_down, g_final):
    w = dict(w_ada=np.asarray(w_ada, np.float32), b_ada=np.asarray(b_ada, np.float32),
             g_attn=np.asarray(g_attn, np.float32), w_in=np.asarray(w_in, np.float32),
             b_fgate=np.asarray(b_fgate, np.float32), g_out_fox=np.asarray(g_out_fox, np.float32),
             g_out_sb=np.asarray(g_out_sb, np.float32), w_out=np.asarray(w_out, np.float32),
             g_mlp=np.asarray(g_mlp, np.float32), w_up=np.asarray(w_up, np.float32),
             conv_w=np.asarray(conv_w, np.float32), conv_b=np.asarray(conv_b, np.float32),
             w_down=np.asarray(w_down, np.float32), g_final=np.asarray(g_final, np.float32))
    return run(x, c, w, n_cores=8)
```

```python
import contextlib
import numpy as np
import concourse.bass as bass
import concourse.mybir as mybir
from concourse.bass_utils import run_bass_kernel_spmd

F32 = mybir.dt.float32
BF16 = mybir.dt.bfloat16
AF = mybir.ActivationFunctionType
ALU = mybir.AluOpType

D = 1024
HD = 64
NH = 16
N_IN = 3080
DFF = 2752
EPS = 1e-6
NG = 22

ENGS = ('pe', 'act', 'dve', 'pool', 'sp')
NDS = 6


class Buf:
    __slots__ = ('w', 'r', 'g', 'name')

    def __init__(self, name=''):
        self.w = []
        self.r = []
        self.g = []
        self.name = name


class Rec:
    __slots__ = ('eng', 'fn', 'deps', 'is_dma', 'sig', 'sem', 'val', 'prewait', 'phase')


class Prog:
    def __init__(self, nc):
        self.nc = nc
        self.phase = 0
        self.lists = {e: [] for e in ENGS}

    def op(self, eng, fn, reads=(), writes=(), dma=False):
        rec = Rec()
        rec.eng = eng
        rec.fn = fn
        rec.is_dma = dma
        rec.sig = dma
        rec.sem = None
        rec.val = 0
        rec.prewait = None
        rec.phase = self.phase
        d2 = []
        seen = set()

        def add(d):
            if d is None or d.phase != self.phase or id(d) in seen:
                return
            if d.eng == 'pe' and eng == 'pe':
                return
            seen.add(id(d))
            d2.append(d)
        for b in reads:
            for wr in b.w:
                add(wr)
        par = {}
        for b in writes:
            p = dma and len(b.w) > 0 and all(wr.is_dma for wr in b.w) and not b.r
            par[id(b)] = p
            if p:
                for g in b.g:
                    add(g)
            else:
                for wr in b.w:
                    add(wr)
                for r in b.r:
                    add(r)
        rec.deps = d2
        for d in d2:
            d.sig = True
        for b in writes:
            if par[id(b)]:
                b.w.append(rec)
            else:
                b.g = list(b.w) + list(b.r)
                b.w = [rec]
                b.r = []
        for b in reads:
            b.r.append(rec)
        self.lists[eng].append(rec)
        return rec

    def run_phase(self):
        nc = self.nc
        lists = self.lists
        ph = self.phase
        with contextlib.ExitStack() as es:
            csem = {e: es.enter_context(nc.semaphore(f"c_{e}_{ph}")) for e in ENGS}
            dsem = {e: ([es.enter_context(nc.semaphore(f"d_{e}_{i}_{ph}")) for i in range(NDS)]
                        if any(r.is_dma for r in lists[e]) else [])
                    for e in ENGS}
            final_tick = {}
            for e in ENGS:
                last = None
                for r in lists[e]:
                    if not r.is_dma:
                        last = r
                if last is not None:
                    last.sig = True
            dcount = {e: 0 for e in ENGS}
            for e in ENGS:
                t = 0
                for r in lists[e]:
                    if r.is_dma:
                        n = dcount[e]
                        s = n % NDS
                        r.sem = dsem[e][s]
                        r.val = 16 * (n // NDS + 1)
                        r.prewait = (dsem[e][s], 16 * (n // NDS)) if n >= NDS else None
                        dcount[e] = n + 1
                    elif r.sig:
                        t += 1
                        r.sem = csem[e]
                        r.val = t
                final_tick[e] = t

            def replay(e, h):
                known = {}

                def wait(sem, val):
                    k = id(sem)
                    if known.get(k, 0) >= val:
                        return
                    known[k] = val
                    h.wait_ge(sem, val)
                for r in lists[e]:
                    for d in r.deps:
                        wait(d.sem, d.val)
                    if r.prewait is not None:
                        wait(*r.prewait)
                    ins = r.fn(h)
                    if r.sig:
                        ins.then_inc(r.sem, 16 if r.is_dma else 1)
                for e2 in ENGS:
                    if final_tick[e2] > 0:
                        wait(csem[e2], final_tick[e2])
                    n = dcount[e2]
                    for s in range(min(n, NDS)):
                        cnt = (n - s + NDS - 1) // NDS
                        wait(dsem[e2][s], 16 * cnt)

            with nc.Block() as block:
                @block.tensor
                def _(h):
                    replay('pe', h)

                @block.scalar
                def _(h):
                    replay('act', h)

                @block.vector
                def _(h):
                    replay('dve', h)

                @block.gpsimd
                def _(h):
                    replay('pool', h)

                @block.sync
                def _(h):
                    replay('sp', h)
        self.lists = {e: [] for e in ENGS}
        self.phase += 1


class Ctx:
    _n = [0]

    def __init__(self, nc, es):
        self.nc = nc
        self.es = es

    def sb(self, shape, dt, name=None):
        Ctx._n[0] += 1
        nm = f"t{Ctx._n[0]}"
        t = self.es.enter_context(self.nc.sbuf_tensor(nm, list(shape), dt))
        return t, Buf(nm)

    def ps(self, dt=F32, name=None):
        Ctx._n[0] += 1
        nm = f"p{Ctx._n[0]}"
        cols = 512 if dt == F32 else 1024
        t = self.es.enter_context(self.nc.psum_tensor(nm, [128, cols], dt))
        return t, Buf(nm)


def build(S, nph=99):
    NT = S // 512
    NB = S // 128
    NSL = NT // 2
    SO = NSL * 512
    SOH = SO + 128
    NHQ = 2 * NSL
    nc = bass.Bass("TRN2", target_bir_lowering=False)

    def din(name, shape, dt=F32):
        return nc.dram_tensor(name, list(shape), dt, kind="ExternalInput").ap()

    def dscr(name, shape, dt):
        import os
        dbg = os.environ.get("MK_DBG", "").split(",")
        return nc.dram_tensor(name, list(shape), dt, kind="ExternalOutput" if name in dbg else "Internal").ap()

    x = din("x", [S, D])
    xo = din("xo", [SOH, D])
    selw = din("selw", [8, 2])
    hvalid = din("hvalid", [128, NHQ])
    ccol = din("ccol", [128, 8])
    w_ada = din("w_ada", [D, 6 * D])
    b_ada = din("b_ada", [1, 6 * D])
    g_attn = din("g_attn", [128, 8])
    w_in = din("w_in", [D, N_IN])
    bfg = din("bfg", [1, 8])
    g_out = din("g_out", [1, D])
    w_out = din("w_out", [D, D])
    g_mlp = din("g_mlp", [128, 8])
    w_up = din("w_up", [D, 2 * DFF])
    cw = din("cw", [128, 2 * NG, 3])
    cb = din("cb", [128, 2 * NG])
    w_down = din("w_down", [DFF, D])
    g_final = din("g_final", [1, D])
    c_ident = din("c_ident", [128, 128])
    c_triu = din("c_triu", [128, 128])
    c_ones = din("c_ones", [128, 128])
    c_trisb = din("c_trisb", [128, 128])
    c_trif = din("c_trif", [128, 2, 128])
    c_tris = din("c_tris", [128, 2, 128])
    c_onehot = din("c_onehot", [8, S])
    c_qmask = din("c_qmask", [8, SOH])
    c_ones3 = din("c_ones3", [3, SOH])
    h_mbias = din("h_mbias", [128, NB, NHQ])
    h_m01 = din("h_m01", [128, NB, NHQ])
    c_zero = din("c_zero", [128, D])
    out = nc.dram_tensor("out", [SO, D], F32, kind="ExternalOutput").ap()

    mod_d = dscr("mod_d", [1, 6 * D], F32)
    Qd = dscr("Qd", [NH, HD, SOH], BF16)
    Kd = dscr("Kd", [NH, HD, S], BF16)
    Vd = dscr("Vd", [NH, 128, NB, HD], BF16)
    RVd = dscr("RVd", [8, SOH], BF16)
    KFd = dscr("KFd", [8, 3, S], BF16)
    Fd = dscr("Fd", [128, 8, NB], F32)
    Rd = dscr("Rd", [128, NSL + 1, 8], F32)
    MIXd = dscr("MIXd", [SOH, D], BF16)
    X1d = dscr("X1d", [SOH, D], F32)
    bd = {n: Buf(n) for n in ['mod_d', 'Qd', 'Kd', 'Vd', 'RVd', 'Fd', 'Rd', 'MIXd', 'X1d', 'out']}

    P = Prog(nc)

    def col_of(off):
        return mod_d[0, off:off + D].rearrange("(c p) -> p c", p=128)

    with contextlib.ExitStack() as es:
        C = Ctx(nc, es)
        cc, b_cc = C.sb([128, 8], F32)
        sc, b_sc = C.sb([128, 8], F32)
        sg0, b_sg0 = C.sb([128, 8], F32)
        wa = [C.sb([128, 8, 512], F32) for _ in range(2)]
        brow, b_brow = C.sb([1, 6 * D], F32)
        mrow, b_mrow = C.sb([1, 6 * D], F32)
        pm = [C.ps(), C.ps()]
        P.op('sp', lambda e: e.dma_start(out=cc[:], in_=ccol), writes=[b_cc], dma=True)
        P.op('sp', lambda e: e.dma_start(out=brow[:], in_=b_ada), writes=[b_brow], dma=True)
        P.op('act', lambda e: e.activation(out=sg0[:], in_=cc[:], func=AF.Sigmoid), reads=[b_cc], writes=[b_sg0])
        P.op('dve', lambda e: e.tensor_tensor(out=sc[:], in0=sg0[:], in1=cc[:], op=ALU.mult),
             reads=[b_sg0, b_cc], writes=[b_sc])
        wav = w_ada.rearrange("(c p) n -> p c n", p=128)
        for n in range(12):
            wt, b_wt = wa[n % 2]
            pt, b_pt = pm[n % 2]
            P.op('sp' if n % 2 == 0 else 'act', lambda e, wt=wt, n=n: e.dma_start(out=wt[:], in_=wav[:, :, n * 512:(n + 1) * 512]),
                 writes=[b_wt], dma=True)
            for kc in range(8):
                P.op('pe', lambda e, pt=pt, wt=wt, kc=kc: e.matmul(pt[0:1, :], lhsT=sc[:, kc:kc + 1], rhs=wt[:, kc, :],
                                                                  start=(kc == 0), stop=(kc == 7)),
                     reads=[b_sc, b_wt], writes=[b_pt])
            P.op('dve', lambda e, pt=pt, n=n: e.tensor_tensor(out=mrow[0:1, n * 512:(n + 1) * 512], in0=pt[0:1, :],
                                                             in1=brow[0:1, n * 512:(n + 1) * 512], op=ALU.add),
                 reads=[b_pt, b_brow], writes=[b_mrow])
        P.op('sp', lambda e: e.dma_start(out=mod_d, in_=mrow[:]), reads=[b_mrow], writes=[bd['mod_d']], dma=True)
        P.run_phase()
        if P.phase >= nph:
            return nc

    with contextlib.ExitStack() as es:
        C = Ctx(nc, es)
        wi, b_wi = C.sb([128, 8, N_IN], BF16)
        idf, b_idf = C.sb([128, 128], F32)
        triu, b_triu = C.sb([128, 128], F32)
        onesf, b_onesf = C.sb([128, 128], F32)
        gcol, b_gcol = C.sb([128, 8], F32)
        scl, b_scl = C.sb([128, 8], F32)
        gsA, b_gsA = C.sb([128, 8], F32)
        shA, b_shA = C.sb([128, 8], F32)
        bf4, b_bf4 = C.sb([128, 4, 8], F32)
        selt, b_selt = C.sb([8, 2], F32)
        xts = [C.sb([128, 4, D], F32) for _ in range(2)]
        xs, b_xs = C.sb([128, 4, D], F32)
        junk, b_junk = C.sb([128, D], BF16)
        ss, b_ss = C.sb([128, 4], F32)
        rstd, b_rstd = C.sb([128, 4], F32)
        hTs = [C.sb([128, 8, 512], BF16) for _ in range(2)]
        qko = [C.sb([128, 512], BF16) for _ in range(4)]
        vsb = [C.sb([128, 4, D], BF16) for _ in range(2)]
        zt, b_zt = C.sb([128, 32], F32)
        et, b_et = C.sb([128, 32], F32)
        spf, b_spf = C.sb([128, 32], F32)
        fblk, b_fblk = C.sb([128, 4, 8], F32)
        carry, b_carry = C.sb([128, 8], F32)
        fneg, b_fneg = C.sb([128, 8, NB], F32)
        rbc, b_rbc = C.sb([128, NSL + 1, 8], F32)
        fte, b_fte = C.sb([8, 512], F32)
        fto, b_fto = C.sb([8, 512], F32)
        fsel, b_fsel = C.sb([8, 512], F32)
        hs, b_hs = C.sb([8, NHQ], F32)
        rv, b_rv = C.sb([8, 512], BF16)
        rvh, b_rvh = C.sb([8, NHQ], BF16)
        kf, b_kf = C.sb([8, 3, 512], BF16)
        kr1, b_kr1 = C.sb([8, 512], F32)
        kr2, b_kr2 = C.sb([8, 512], F32)
        psT = [C.ps() for _ in range(2)]
        psP = [C.ps() for _ in range(3)]
        psF, b_psF = C.ps()
        psC, b_psC = C.ps()
        psFT, b_psFT = C.ps()

        wiv = w_in.rearrange("(c p) n -> p c n", p=128)
        for k4 in range(4):
            lo, hi = k4 * 770, (k4 + 1) * 770
            P.op('pool', lambda e, lo=lo, hi=hi: e.dma_start(out=wi[:, :, lo:hi], in_=wiv[:, :, lo:hi]),
                 writes=[b_wi], dma=True)
        P.op('sp', lambda e: e.dma_start(out=idf[:], in_=c_ident), writes=[b_idf], dma=True)
        P.op('sp', lambda e: e.dma_start(out=triu[:], in_=c_triu), writes=[b_triu], dma=True)
        P.op('sp', lambda e: e.dma_start(out=onesf[:], in_=c_ones), writes=[b_onesf], dma=True)
        P.op('sp', lambda e: e.dma_start(out=gcol[:], in_=g_attn), writes=[b_gcol], dma=True)
        P.op('sp', lambda e: e.dma_start(out=selt[:], in_=selw), writes=[b_selt], dma=True)
        P.op('sp', lambda e: e.dma_start(out=shA[:], in_=col_of(0), allow_slow_non_contiguous=True),
             reads=[bd['mod_d']], writes=[b_shA], dma=True)
        P.op('sp', lambda e: e.dma_start(out=scl[:], in_=col_of(D), allow_slow_non_contiguous=True),
             reads=[bd['mod_d']], writes=[b_scl], dma=True)
        for j in range(4):
            P.op('sp', lambda e, j=j: e.dma_start(out=bf4[:, j, :], in_=bfg[0, :].partition_broadcast(128)),
                 writes=[b_bf4], dma=True)
        P.op('dve', lambda e: e.tensor_scalar(out=scl[:], in0=scl[:], scalar1=1.0, scalar2=None, op0=ALU.add),
             reads=[b_scl], writes=[b_scl])
        P.op('dve', lambda e: e.tensor_tensor(out=gsA[:], in0=scl[:], in1=gcol[:], op=ALU.mult),
             reads=[b_scl, b_gcol], writes=[b_gsA])
        P.op('dve', lambda e: e.memset(carry[:], 0.0), writes=[b_carry])
        P.op('dve', lambda e: e.memset(fto[:], 0.0), writes=[b_fto])

        xv = x.rearrange("(t j p) d -> t p j d", j=4, p=128)
        xov = xo[0:SO, :].rearrange("(t j p) d -> t p j d", j=4, p=128)
        Qv = Qd.rearrange("h d s -> (h d) s")
        Kv = Kd.rearrange("h d s -> (h d) s")
        cn = {'qk': 0, 'pn': 0}

        work = [('full', t) for t in range(NT)] + [('own', i) for i in range(NSL)] + [('halo', 0)]

        def src_of(w):
            kind, idx = w
            if kind == 'full':
                return xv[idx], 4
            if kind == 'own':
                return xov[idx], 4
            return xo[SO:SO + 128, :], 1

        def issue_load(wn):
            srcap, nj = src_of(work[wn])
            xt, b_xt = xts[wn % 2]
            if nj == 4:
                P.op('sp', lambda e: e.dma_start(out=xt[:], in_=srcap), writes=[b_xt], dma=True)
            else:
                P.op('sp', lambda e: e.dma_start(out=xt[:, 0, :], in_=srcap), writes=[b_xt], dma=True)

        issue_load(0)
        for wn, (kind, idx) in enumerate(work):
            nj = 1 if kind == 'halo' else 4
            NTK = 128 * nj
            xt, b_xt = xts[wn % 2]
            hT, b_hT = hTs[wn % 2]
            if wn + 1 < len(work):
                issue_load(wn + 1)
            for j in range(nj):
                P.op('act', lambda e, xt=xt, j=j: e.activation(out=junk[:], in_=xt[:, j, :], func=AF.Square,
                                                              accum_out=ss[:, j:j + 1]),
                     reads=[b_xt], writes=[b_junk, b_ss])
            P.op('act', lambda e, nj=nj: e.activation(out=rstd[:, 0:nj], in_=ss[:, 0:nj], func=AF.Ln, scale=1.0 / D, bias=EPS),
                 reads=[b_ss], writes=[b_rstd])
            P.op('act', lambda e, nj=nj: e.activation(out=rstd[:, 0:nj], in_=rstd[:, 0:nj], func=AF.Exp, scale=-0.5),
                 reads=[b_rstd], writes=[b_rstd])
            for j in range(nj):
                if j % 2 == 0:
                    P.op('dve', lambda e, xt=xt, j=j: e.tensor_scalar(out=xs[:, j, :], in0=xt[:, j, :], scalar1=rstd[:, j:j + 1],
                                                                     scalar2=None, op0=ALU.mult),
                         reads=[b_xt, b_rstd], writes=[b_xs])
                else:
                    P.op('act', lambda e, xt=xt, j=j: e.activation(out=xs[:, j, :], in_=xt[:, j, :], func=AF.Copy,
                                                                  scale=rstd[:, j:j + 1]),
                         reads=[b_xt, b_rstd], writes=[b_xs])
            for c in range(8):
                pt, b_pt = psT[c % 2]
                for j in range(nj):
                    P.op('pe', lambda e, pt=pt, j=j, c=c: e.transpose(out=pt[:, j * 128:(j + 1) * 128],
                                                                     in_=xs[:, j, c * 128:(c + 1) * 128], identity=idf[:]),
                         reads=[b_xs, b_idf], writes=[b_pt])
                if c % 2 == 0:
                    P.op('act', lambda e, pt=pt, hT=hT, c=c, NTK=NTK: e.activation(out=hT[:, c, 0:NTK], in_=pt[:, 0:NTK], func=AF.Identity,
                                                                                  scale=gsA[:, c:c + 1], bias=shA[:, c:c + 1]),
                         reads=[b_pt, b_gsA, b_shA], writes=[b_hT])
                else:
                    P.op('dve', lambda e, pt=pt, hT=hT, c=c, NTK=NTK: e.tensor_scalar(out=hT[:, c, 0:NTK], in0=pt[:, 0:NTK], scalar1=gsA[:, c:c + 1],
                                                                                     scalar2=shA[:, c:c + 1], op0=ALU.mult, op1=ALU.add),
                         reads=[b_pt, b_gsA, b_shA], writes=[b_hT])
            for Pp in range(8):
                if kind == 'full':
                    col0 = 512 + 128 * Pp if Pp < 4 else 2048 + 128 * (Pp - 4)
                else:
                    col0 = 128 * Pp if Pp < 4 else 1536 + 128 * (Pp - 4)
                pp, b_pp = psP[cn['pn'] % 3]
                cn['pn'] += 1
                ot, b_ot = qko[cn['qk'] % 4]
                cn['qk'] += 1
                for c in range(8):
                    P.op('pe', lambda e, pp=pp, hT=hT, c=c, col0=col0, NTK=NTK: e.matmul(pp[:, 0:NTK], lhsT=wi[:, c, col0:col0 + 128],
                                                                                        rhs=hT[:, c, 0:NTK], start=(c == 0), stop=(c == 7)),
                         reads=[b_wi, b_hT], writes=[b_pp])
                if kind == 'full':
                    P.op('dve', lambda e, pp=pp, ot=ot: e.tensor_copy(out=ot[:], in_=pp[:]), reads=[b_pp], writes=[b_ot])
                    P.op('sp', lambda e, ot=ot, Pp=Pp, idx=idx: e.dma_start(out=Kv[Pp * 128:(Pp + 1) * 128, idx * 512:(idx + 1) * 512], in_=ot[:]),
                         reads=[b_ot], dma=True)
                else:
                    q0 = idx * 512 if kind == 'own' else SO
                    P.op('act', lambda e, pp=pp, ot=ot, NTK=NTK: e.activation(out=ot[:, 0:NTK], in_=pp[:, 0:NTK], func=AF.Copy, scale=0.125),
                         reads=[b_pp], writes=[b_ot])
                    P.op('sp', lambda e, ot=ot, Pp=Pp, q0=q0, NTK=NTK: e.dma_start(out=Qv[Pp * 128:(Pp + 1) * 128, q0:q0 + NTK], in_=ot[:, 0:NTK]),
                         reads=[b_ot], dma=True)
            if kind != 'full':
                continue
            t = idx
            vt, b_vt = vsb[t % 2]
            for j in range(4):
                for half in range(2):
                    vcol = 1024 if half == 0 else 2560
                    pp, b_pp = psP[cn['pn'] % 3]
                    cn['pn'] += 1
                    for c in range(8):
                        P.op('pe', lambda e, pp=pp, hT=hT, c=c, j=j, vcol=vcol: e.matmul(pp[:], lhsT=hT[:, c, j * 128:(j + 1) * 128],
                                                                                        rhs=wi[:, c, vcol:vcol + 512],
                                                                                        start=(c == 0), stop=(c == 7)),
                             reads=[b_wi, b_hT], writes=[b_pp])
                    if (j + half) % 2 == 0:
                        P.op('act', lambda e, pp=pp, vt=vt, j=j, half=half: e.activation(out=vt[:, j, half * 512:(half + 1) * 512],
                                                                                        in_=pp[:], func=AF.Copy),
                             reads=[b_pp], writes=[b_vt])
                    else:
                        P.op('dve', lambda e, pp=pp, vt=vt, j=j, half=half: e.tensor_copy(out=vt[:, j, half * 512:(half + 1) * 512],
                                                                                         in_=pp[:]),
                             reads=[b_pp], writes=[b_vt])
            for H in range(NH):
                P.op('sp', lambda e, vt=vt, H=H, t=t: e.dma_start(out=Vd[H][:, 4 * t:4 * t + 4, :], in_=vt[:, :, HD * H:HD * (H + 1)]),
                     reads=[b_vt], dma=True)
            for j in range(4):
                for c in range(8):
                    P.op('pe', lambda e, hT=hT, c=c, j=j: e.matmul(psF[:, j * 8:(j + 1) * 8], lhsT=hT[:, c, j * 128:(j + 1) * 128],
                                                                  rhs=wi[:, c, 3072:3080], start=(c == 0), stop=(c == 7)),
                         reads=[b_wi, b_hT], writes=[b_psF])
            P.op('dve', lambda e: e.tensor_tensor(out=zt[:], in0=psF[:, 0:32], in1=bf4[:].rearrange("p a b -> p (a b)"), op=ALU.add),
                 reads=[b_psF, b_bf4], writes=[b_zt])
            P.op('act', lambda e: e.activation(out=et[:], in_=zt[:], func=AF.Exp, scale=-1.0), reads=[b_zt], writes=[b_et])
            P.op('act', lambda e: e.activation(out=spf[:], in_=et[:], func=AF.Ln, bias=1.0), reads=[b_et], writes=[b_spf])
            for j in range(4):
                P.op('pe', lambda e, j=j: e.matmul(psC[:, 0:8], lhsT=triu[:], rhs=spf[:, j * 8:(j + 1) * 8], start=True, stop=True),
                     reads=[b_triu, b_spf], writes=[b_psC])
                P.op('pe', lambda e, j=j: e.matmul(psC[:, 8:16], lhsT=onesf[:], rhs=spf[:, j * 8:(j + 1) * 8], start=True, stop=True),
                     reads=[b_onesf, b_spf], writes=[b_psC])
                P.op('dve', lambda e, j=j: e.tensor_tensor(out=fblk[:, j, :], in0=psC[:, 0:8], in1=carry[:], op=ALU.add),
                     reads=[b_psC, b_carry], writes=[b_fblk])
                P.op('dve', lambda e: e.tensor_tensor(out=carry[:], in0=psC[:, 8:16], in1=carry[:], op=ALU.add),
                     reads=[b_psC, b_carry], writes=[b_carry])
                P.op('pool', lambda e, j=j, t=t: e.tensor_copy(out=fneg[:, :, 4 * t + j], in_=fblk[:, j, :]),
                     reads=[b_fblk], writes=[b_fneg])
            for j in range(4):
                P.op('pe', lambda e, j=j: e.transpose(out=psFT[0:8, j * 128:(j + 1) * 128], in_=fblk[:, j, :], identity=idf[:]),
                     reads=[b_fblk, b_idf], writes=[b_psFT])
            i = t // 2
            fcur, b_fcur = (fte, b_fte) if t % 2 == 0 else (fto, b_fto)

            def emit_split(fcur=fcur, b_fcur=b_fcur, t=t):
                P.op('dve', lambda e: e.tensor_copy(out=kf[:, 0, :], in_=fcur[:]), reads=[b_fcur], writes=[b_kf])
                P.op('dve', lambda e: e.tensor_tensor(out=kr1[:], in0=fcur[:], in1=kf[:, 0, :], op=ALU.subtract),
                     reads=[b_fcur, b_kf], writes=[b_kr1])
                P.op('dve', lambda e: e.tensor_copy(out=kf[:, 1, :], in_=kr1[:]), reads=[b_kr1], writes=[b_kf])
                P.op('dve', lambda e: e.tensor_tensor(out=kr2[:], in0=kr1[:], in1=kf[:, 1, :], op=ALU.subtract),
                     reads=[b_kr1, b_kf], writes=[b_kr2])
                P.op('dve', lambda e: e.tensor_copy(out=kf[:, 2, :], in_=kr2[:]), reads=[b_kr2], writes=[b_kf])
                P.op('sp', lambda e: e.dma_start(out=KFd[:, :, t * 512:(t + 1) * 512], in_=kf[:]), reads=[b_kf], dma=True)
            if t % 2 == 0:
                P.op('act', lambda e: e.activation(out=fte[:], in_=psFT[0:8, :], func=AF.Copy), reads=[b_psFT], writes=[b_fte])
                emit_split()
                P.op('dve', lambda e, i=i: e.tensor_scalar(out=hs[:, 2 * i:2 * i + 2], in0=fto[:, 510:512], scalar1=selt[:, 0:1],
                                                          scalar2=None, op0=ALU.mult), reads=[b_fto, b_selt], writes=[b_hs])
                P.op('dve', lambda e, i=i: e.scalar_tensor_tensor(out=hs[:, 2 * i:2 * i + 2], in0=fte[:, 510:512], scalar=selt[:, 1:2],
                                                                 in1=hs[:, 2 * i:2 * i + 2], op0=ALU.mult, op1=ALU.add),
                     reads=[b_fte, b_selt, b_hs], writes=[b_hs])
            else:
                P.op('act', lambda e: e.activation(out=fto[:], in_=psFT[0:8, :], func=AF.Copy), reads=[b_psFT], writes=[b_fto])
                emit_split()
                P.op('pool', lambda e, i=i: e.tensor_copy(out=rbc[:, i, :], in_=carry[:]), reads=[b_carry], writes=[b_rbc])
                P.op('dve', lambda e: e.tensor_scalar(out=fsel[:], in0=fte[:], scalar1=selt[:, 0:1], scalar2=None, op0=ALU.mult),
                     reads=[b_fte, b_selt], writes=[b_fsel])
                P.op('dve', lambda e: e.scalar_tensor_tensor(out=fsel[:], in0=fto[:], scalar=selt[:, 1:2], in1=fsel[:],
                                                             op0=ALU.mult, op1=ALU.add), reads=[b_fto, b_selt, b_fsel], writes=[b_fsel])
                P.op('dve', lambda e: e.tensor_scalar(out=rv[:], in0=fsel[:], scalar1=-1.0, scalar2=None, op0=ALU.mult),
                     reads=[b_fsel], writes=[b_rv])
                P.op('sp', lambda e, i=i: e.dma_start(out=RVd[:, i * 512:(i + 1) * 512], in_=rv[:]), reads=[b_rv], dma=True)
                if t == NT - 1:
                    P.op('pool', lambda e: e.tensor_copy(out=rbc[:, NSL, :], in_=carry[:]), reads=[b_carry], writes=[b_rbc])
                    P.op('dve', lambda e: e.tensor_scalar(out=rvh[:], in0=hs[:], scalar1=-1.0, scalar2=None, op0=ALU.mult),
                         reads=[b_hs], writes=[b_rvh])
                    P.op('sp', lambda e: e.dma_start(out=RVd[:, SO:SO + NHQ], in_=rvh[:]), reads=[b_rvh], dma=True)
                    P.op('sp', lambda e: e.dma_start(out=Fd, in_=fneg[:]), reads=[b_fneg], writes=[bd['Fd']], dma=True)
                    P.op('sp', lambda e: e.dma_start(out=Rd, in_=rbc[:]), reads=[b_rbc], writes=[bd['Rd']], dma=True)
        P.run_phase()
        if P.phase >= nph:
            return nc

    with contextlib.ExitStack() as es:
        C = Ctx(nc, es)
        kaug = [C.sb([76, S], BF16) for _ in range(4)]
        qaug = [C.sb([76, SOH], BF16) for _ in range(4)]
        vts = [C.sb([128, NB, 65], BF16) for _ in range(4)]
        trisb, b_trisb = C.sb([128, 128], BF16)
        idb, b_idb = C.sb([128, 128], BF16)
        trif, b_trif = C.sb([128, 2, 128], BF16)
        tris, b_tris = C.sb([128, 2, 128], BF16)
        hmb, b_hmb = C.sb([128, NB, NHQ], BF16)
        hm01, b_hm01 = C.sb([128, NB, NHQ], BF16)
        onec, b_onec = C.sb([128, 1], BF16)
        gbc, b_gbc = C.sb([128, D], F32)
        fneg, b_fneg = C.sb([128, 8, NB], F32)
        rbc, b_rbc = C.sb([128, NSL + 1, 8], F32)
        zer, b_zer = C.sb([128, 256], BF16)
        biases = [C.sb([128, NB], F32) for _ in range(2)]
        pts = [C.sb([128, 512], BF16) for _ in range(3)]
        ats = [C.sb([128, 512], BF16) for _ in range(3)]
        Es = [C.sb([128, 512], F32) for _ in range(3)]
        sps = [C.sb([128, 512], BF16) for _ in range(3)]
        eCs = [C.sb([128, 512], F32) for _ in range(2)]
        carrs = [C.sb([128, 4], F32) for _ in range(2)]
        ecar = [C.sb([128, 4], F32) for _ in range(4)]
        oaccs = [C.sb([128, 4, HD], F32) for _ in range(2)]
        osbs = [C.sb([128, 4, HD], F32) for _ in range(2)]
        rtmp, b_rtmp = C.sb([128, 4, HD], F32)
        rden, b_rden = C.sb([128, 4], F32)
        ssq, b_ssq = C.sb([128, 4], F32)
        rn, b_rn = C.sb([128, 4], F32)
        junk2, b_junk2 = C.sb([128, HD], F32)
        ys = [C.sb([128, 4, HD], BF16) for _ in range(2)]
        ysF = [C.sb([128, 4, HD], BF16) for _ in range(2)]
        psS = [C.ps() for _ in range(3)]
        psCb = [C.ps() for _ in range(2)]
        psO = [C.ps() for _ in range(2)]
        psFo = C.ps()

        P.op('pool', lambda e: e.dma_start(out=trisb[:], in_=c_trisb), writes=[b_trisb], dma=True)
        P.op('pool', lambda e: e.dma_start(out=idb[:], in_=c_ident), writes=[b_idb], dma=True)
        P.op('pool', lambda e: e.dma_start(out=trif[:], in_=c_trif), writes=[b_trif], dma=True)
        P.op('pool', lambda e: e.dma_start(out=tris[:], in_=c_tris), writes=[b_tris], dma=True)
        P.op('pool', lambda e: e.dma_start(out=hmb[:], in_=h_mbias), writes=[b_hmb], dma=True)
        P.op('pool', lambda e: e.dma_start(out=hm01[:], in_=h_m01), writes=[b_hm01], dma=True)
        P.op('pool', lambda e: e.dma_start(out=zer[:], in_=c_zero[:, 0:256]), writes=[b_zer], dma=True)
        for z4 in range(4):
            P.op('sp', lambda e, z4=z4: e.dma_start(out=MIXd[SO + NHQ:SOH, 256 * z4:256 * (z4 + 1)], in_=zer[0:128 - NHQ, :]),
                 reads=[b_zer], dma=True)
        P.op('sp', lambda e: e.dma_start(out=gbc[:], in_=g_out[0, :].partition_broadcast(128)), writes=[b_gbc], dma=True)
        P.op('dve', lambda e: e.memset(onec[:], 1.0), writes=[b_onec])
        for i2 in range(2):
            ka, b_ka = kaug[i2]
            vt, b_vt = vts[i2]
            P.op('dve', lambda e, ka=ka: e.memset(ka[64:65, :], 1.0), writes=[b_ka])
            P.op('pool', lambda e, vt=vt: e.memset(vt[:, :, 64:65], 1.0), writes=[b_vt])
        for i4 in range(4):
            ka, b_ka = kaug[i4]
            qa, b_qa = qaug[i4]
            rb = 68 if i4 < 2 else 64
            if i4 < 2:
                P.op('pool', lambda e, qa=qa: e.dma_start(out=qa[65:68, :], in_=c_ones3), writes=[b_qa], dma=True)
            P.op('pool', lambda e, ka=ka, rb=rb: e.dma_start(out=ka[rb:rb + 8, :], in_=c_onehot), writes=[b_ka], dma=True)
            P.op('pool', lambda e, qa=qa, rb=rb: e.dma_start(out=qa[rb:rb + 8, :], in_=c_qmask), writes=[b_qa], dma=True)

        def hset(Hn):
            return (Hn % 2) if Hn < 8 else 2 + (Hn % 2)

        MIXv = MIXd[0:SO, :].rearrange("(t m p) c -> t p m c", m=4, p=128)

        def load_head(Hn):
            ka, b_ka = kaug[hset(Hn)]
            qa, b_qa = qaug[hset(Hn)]
            vt, b_vt = vts[hset(Hn)]
            P.op('sp', lambda e: e.dma_start(out=ka[0:64, :], in_=Kd[Hn]), reads=[bd['Kd']], writes=[b_ka], dma=True)
            P.op('sp', lambda e: e.dma_start(out=qa[0:64, :], in_=Qd[Hn]), reads=[bd['Qd']], writes=[b_qa], dma=True)
            if Hn < 8:
                P.op('sp', lambda e: e.dma_start(out=qa[64:65, :], in_=RVd[Hn:Hn + 1, :]),
                     reads=[bd['RVd']], writes=[b_qa], dma=True)
                P.op('sp', lambda e: e.dma_start(out=ka[65:68, :], in_=KFd[Hn]), writes=[b_ka], dma=True)
            P.op('sp', lambda e: e.dma_start(out=vt[:, :, 0:64], in_=Vd[Hn]), reads=[bd['Vd']], writes=[b_vt], dma=True)

        def slot_desc(sl):
            if sl < NSL:
                return dict(q0=512 * sl, NQ=512, MB=4, QP=128, nkb=8 * sl + 8, km=8 * sl, halo=False, sl=sl)
            return dict(q0=SO, NQ=NHQ, MB=1, QP=NHQ, nkb=NB, km=0, halo=True, sl=sl)

        def epilogue(src, b_src, yt, b_yt, H, sd):
            MB, QP = sd['MB'], sd['QP']
            for m in range(MB):
                P.op('act', lambda e, m=m: e.activation(out=junk2[0:QP, :], in_=src[0:QP, m, :], func=AF.Square,
                                                        accum_out=ssq[0:QP, m:m + 1]),
                     reads=[b_src], writes=[b_junk2, b_ssq])
            P.op('act', lambda e: e.activation(out=rn[0:QP, 0:MB], in_=ssq[0:QP, 0:MB], func=AF.Ln, scale=1.0 / HD, bias=EPS),
                 reads=[b_ssq], writes=[b_rn])
            P.op('act', lambda e: e.activation(out=rn[0:QP, 0:MB], in_=rn[0:QP, 0:MB], func=AF.Exp, scale=-0.5),
                 reads=[b_rn], writes=[b_rn])
            for m in range(MB):
                P.op('dve', lambda e, m=m: e.scalar_tensor_tensor(
                    out=yt[0:QP, m, :], in0=src[0:QP, m, :], scalar=rn[0:QP, m:m + 1], in1=gbc[0:QP, HD * H:HD * (H + 1)],
                    op0=ALU.mult, op1=ALU.mult), reads=[b_src, b_rn, b_gbc], writes=[b_yt])
            if sd['halo']:
                P.op('sp', lambda e: e.dma_start(out=MIXd[SO:SO + NHQ, HD * H:HD * (H + 1)], in_=yt[0:NHQ, 0, :]),
                     reads=[b_yt], dma=True)
            else:
                P.op('sp', lambda e: e.dma_start(out=MIXv[sd['sl']][:, :, HD * H:HD * (H + 1)], in_=yt[:]),
                     reads=[b_yt], dma=True)

        iters = []
        defer = []
        defer_epi = []
        tails = []
        step_now = [0]
        cnt = {'S': 0, 'P': 0, 'tile': 0, 'sbn': 0, 'ftile': 0, 'stile': 0}

        def fox_iter(ftl, pos, H, sd, kb, first_head_iter, ka, b_ka, qa, b_qa, vt, b_vt, bi, b_bi, po, b_po, yt, b_yt):
            pS, b_pS = psS[pos % 3]
            pt, b_pt = pts[cnt['P'] % 3]
            cnt['P'] += 1
            pov = po[:].rearrange("p (m c) -> p m c", c=128)
            q0, NQ, MB, QP, nkb = sd['q0'], sd['NQ'], sd['MB'], sd['QP'], sd['nkb']
            masked = kb >= sd['km']
            sl = sd['sl']

            def st0():
                KR = 76 if (masked and not sd['halo']) else 68
                P.op('pe', lambda e: e.matmul(pS[:, 0:NQ], lhsT=ka[0:KR, kb * 128:(kb + 1) * 128], rhs=qa[0:KR, q0:q0 + NQ],
                                              start=True, stop=not masked), reads=[b_ka, b_qa], writes=[b_pS])
                if masked:
                    if sd['halo']:
                        P.op('pe', lambda e: e.matmul(pS[:, 0:NQ], lhsT=idb[:], rhs=hmb[:, kb, :], start=False, stop=True),
                             reads=[b_idb, b_hmb], writes=[b_pS])
                    else:
                        jj = kb - sd['km']
                        mc = 128 * (jj % 4)
                        P.op('pe', lambda e: e.matmul(pS[:, mc:mc + 128], lhsT=idb[:], rhs=trif[:, jj // 4, :], start=False, stop=True),
                             reads=[b_idb, b_trif], writes=[b_pS])

            def st1():
                P.op('act', lambda e: e.activation(out=pt[:, 0:NQ], in_=pS[:, 0:NQ], func=AF.Exp),
                     reads=[b_pS], writes=[b_pt])

            def st2():
                for m in range(MB):
                    P.op('pe', lambda e, m=m: e.matmul(pov[0:QP, m, 0:65], lhsT=pt[:, m * 128:m * 128 + QP], rhs=vt[:, kb, 0:65],
                                                       start=(kb == 0 and m == 0), stop=(kb == nkb - 1)),
                         reads=[b_pt, b_vt], writes=[b_po])
                if first_head_iter and H + 1 < 8:
                    defer.append((step_now[0] + 5, H + 1))
                    defer.append((step_now[0] + 5, H + 9))
                if kb == nkb - 1:
                    osb, b_osb = osbs[ftl % 2]
                    P.op('dve', lambda e: e.reciprocal(out=rden[0:QP, 0:MB], in_=pov[0:QP, 0:MB, 64]), reads=[b_po], writes=[b_rden])
                    for m in range(MB):
                        P.op('dve', lambda e, m=m: e.tensor_scalar(out=osb[0:QP, m, :], in0=pov[0:QP, m, 0:64],
                                                                   scalar1=rden[0:QP, m:m + 1], scalar2=None, op0=ALU.mult),
                             reads=[b_po, b_rden], writes=[b_osb])
                    defer_epi.append((step_now[0] + 3, (osb, b_osb, yt, b_yt, H, sd)))
            return [st0, st1, st2]

        def sb_iter(pos, H, sd, kb, first, last, first_head_iter, ka, b_ka, qa, b_qa, vt, b_vt, carr, b_carr, oacc, b_oacc, yt, b_yt):
            pS, b_pS = psS[pos % 3]
            n = cnt['sbn']
            cnt['sbn'] += 1
            pC, b_pC = psCb[n % 2]
            pv, b_pv = psO[n % 2]
            pcs, b_pcs = pv[:, 64:68], b_pv
            pvv = pv[:].rearrange("p (m c) -> p m c", c=128)
            Et, b_Et = Es[n % 3]
            st, b_st = sps[n % 3]
            eC, b_eC = eCs[n % 2]
            ec, b_ec = ecar[n % 4]
            at, b_at = ats[n % 3]
            q0, NQ, MB, QP, nkb = sd['q0'], sd['NQ'], sd['MB'], sd['QP'], sd['nkb']
            masked = kb >= sd['km']

            def s0():
                KR = 72 if (masked and not sd['halo']) else 64
                P.op('pe', lambda e: e.matmul(pS[:, 0:NQ], lhsT=ka[0:KR, kb * 128:(kb + 1) * 128], rhs=qa[0:KR, q0:q0 + NQ],
                                              start=True, stop=not masked), reads=[b_ka, b_qa], writes=[b_pS])
                if masked:
                    if sd['halo']:
                        P.op('pe', lambda e: e.matmul(pS[:, 0:NQ], lhsT=idb[:], rhs=hm01[:, kb, :], start=False, stop=True),
                             reads=[b_idb, b_hm01], writes=[b_pS])
                    else:
                        jj = kb - sd['km']
                        mc = 128 * (jj % 4)
                        P.op('pe', lambda e: e.matmul(pS[:, mc:mc + 128], lhsT=idb[:], rhs=tris[:, jj // 4, :], start=False, stop=True),
                             reads=[b_idb, b_tris], writes=[b_pS])

            def s1():
                P.op('act', lambda e: e.activation(out=Et[:, 0:NQ], in_=pS[:, 0:NQ], func=AF.Exp), reads=[b_pS], writes=[b_Et])
                tails.append(lambda: P.op('act', lambda e: e.activation(out=st[:, 0:NQ], in_=Et[:, 0:NQ], func=AF.Ln, bias=1.0),
                                          reads=[b_Et], writes=[b_st]))

            def s2():
                P.op('pe', lambda e: e.matmul(pC[:, 0:NQ], lhsT=trisb[:], rhs=st[:, 0:NQ], start=True, stop=True),
                     reads=[b_trisb, b_st], writes=[b_pC])
                for m in range(MB):
                    P.op('pe', lambda e, m=m: e.matmul(pcs[0:QP, m:m + 1], lhsT=st[:, m * 128:m * 128 + QP], rhs=onec[:],
                                                       start=True, stop=True), reads=[b_st, b_onec], writes=[b_pcs])

            def s3():
                if first:
                    P.op('pool', lambda e: e.memset(carr[:], 0.0), writes=[b_carr])
                    P.op('pool', lambda e: e.memset(oacc[:], 0.0), writes=[b_oacc])
                P.op('act', lambda e: e.activation(out=eC[:, 0:NQ], in_=pC[:, 0:NQ], func=AF.Exp, scale=-1.0), reads=[b_pC], writes=[b_eC])
                P.op('act', lambda e: e.activation(out=ec[0:QP, 0:MB], in_=carr[0:QP, 0:MB], func=AF.Exp, scale=-1.0),
                     reads=[b_carr], writes=[b_ec])
                P.op('dve', lambda e: e.tensor_tensor(out=at[:, 0:NQ], in0=Et[:, 0:NQ], in1=eC[:, 0:NQ], op=ALU.mult),
                     reads=[b_Et, b_eC], writes=[b_at])
                P.op('dve', lambda e: e.tensor_tensor(out=carr[0:QP, 0:MB], in0=pcs[0:QP, 0:MB], in1=carr[0:QP, 0:MB], op=ALU.add),
                     reads=[b_pcs, b_carr], writes=[b_carr])

            def s4():
                for m in range(MB):
                    P.op('pe', lambda e, m=m: e.matmul(pvv[0:QP, m, 0:64], lhsT=at[:, m * 128:m * 128 + QP], rhs=vt[:, kb, 0:64],
                                                       start=True, stop=True), reads=[b_at, b_vt], writes=[b_pv])

            def s5():
                if MB == 4:
                    P.op('dve', lambda e: e.tensor_tensor(out=rtmp[:], in0=pvv[:, :, 0:64],
                                                          in1=ec[:, 0:4].unsqueeze(2).to_broadcast([128, 4, HD]), op=ALU.mult),
                         reads=[b_pv, b_ec], writes=[b_rtmp])
                    P.op('dve', lambda e: e.tensor_tensor(out=oacc[:], in0=oacc[:], in1=rtmp[:], op=ALU.add),
                         reads=[b_rtmp, b_oacc], writes=[b_oacc])
                else:
                    for m in range(MB):
                        P.op('dve', lambda e, m=m: e.scalar_tensor_tensor(
                            out=oacc[0:QP, m, :], in0=pvv[0:QP, m, 0:64], scalar=ec[0:QP, m:m + 1], in1=oacc[0:QP, m, :],
                            op0=ALU.mult, op1=ALU.add), reads=[b_pv, b_ec, b_oacc], writes=[b_oacc])
                if last:
                    defer_epi.append((step_now[0] + 3, (oacc, b_oacc, yt, b_yt, H, sd)))
            return [s0, s1, s2, s3, s4, s5]

        load_head(0)
        load_head(8)
        for Hp in range(8):
            lists = []
            for H in (Hp, 8 + Hp):
                fox = H < 8
                ka, b_ka = kaug[hset(H)]
                qa, b_qa = qaug[hset(H)]
                vt, b_vt = vts[hset(H)]
                fh = fox
                li = []
                for sl in range(NSL + 1):
                    sd = slot_desc(sl)
                    nkb = sd['nkb']
                    if fox:
                        tl = cnt['ftile']
                        cnt['ftile'] += 1
                        yt, b_yt = ysF[tl % 2]
                        bi, b_bi = biases[tl % 2]
                        po, b_po = psFo
                        for kb in range(nkb):
                            li.append(fox_iter(tl, cnt['S'] + 2 * len(li), H, sd, kb, fh, ka, b_ka, qa, b_qa, vt, b_vt, bi, b_bi, po, b_po, yt, b_yt))
                            fh = False
                    else:
                        tl = cnt['stile']
                        cnt['stile'] += 1
                        yt, b_yt = ys[tl % 2]
                        carr, b_carr = carrs[tl % 2]
                        oacc, b_oacc = oaccs[tl % 2]
                        order = list(reversed(range(nkb)))
                        for idx, kb in enumerate(order):
                            li.append(sb_iter(cnt['S'] + 2 * len(li) + 1, H, sd, kb, idx == 0, idx == nkb - 1, False, ka, b_ka, qa, b_qa, vt, b_vt,
                                              carr, b_carr, oacc, b_oacc, yt, b_yt))
                lists.append(li)
            assert len(lists[0]) == len(lists[1])
            cnt['S'] += 2 * len(lists[0])
            for fa, sa in zip(lists[0], lists[1]):
                iters.append(fa)
                iters.append(sa)
        nsteps = len(iters) + 12
        for step in range(nsteps):
            step_now[0] = step
            for (ds, Hn) in list(defer):
                if ds <= step:
                    defer.remove((ds, Hn))
                    load_head(Hn)
            for item in list(defer_epi):
                if item[0] <= step:
                    defer_epi.remove(item)
                    epilogue(*item[1])
            for j in range(6):
                n = step - j
                if 0 <= n < len(iters) and j < len(iters[n]):
                    iters[n][j]()
            for tfn in tails:
                tfn()
            tails.clear()
        P.run_phase()
        if P.phase >= nph:
            return nc

    with contextlib.ExitStack() as es:
        C = Ctx(nc, es)
        wo, b_wo = C.sb([128, 8, D], BF16)
        idb, b_idb = C.sb([128, 128], BF16)
        gA, b_gA = C.sb([128, D], F32)
        xts = [C.sb([128, 4, D], F32) for _ in range(2)]
        mxs = [C.sb([128, 4, D], BF16) for _ in range(2)]
        mT, b_mT = C.sb([128, 8, 512], BF16)
        tmps = [C.sb([128, 512], F32) for _ in range(2)]
        psT = [C.ps() for _ in range(2)]
        psP = [C.ps() for _ in range(3)]
        P.op('pool', lambda e: e.dma_start(out=wo[:], in_=w_out.rearrange("(c p) n -> p c n", p=128)), writes=[b_wo], dma=True)
        P.op('pool', lambda e: e.dma_start(out=idb[:], in_=c_ident), writes=[b_idb], dma=True)
        P.op('sp', lambda e: e.dma_start(out=gA[:], in_=mod_d[0, 2 * D:3 * D].partition_broadcast(128)),
             reads=[bd['mod_d']], writes=[b_gA], dma=True)
        work = [(i * 512, 4) for i in range(NSL)] + [(SO, 1)]

        def load3a(wn):
            r0, nj = work[wn]
            xt, b_xt = xts[wn % 2]
            mx, b_mx = mxs[wn % 2]
            xsrc = xo[r0:r0 + 128 * nj, :].rearrange("(j p) d -> p j d", p=128)
            msrc = MIXd[r0:r0 + 128 * nj, :].rearrange("(j p) d -> p j d", p=128)
            P.op('sp', lambda e: e.dma_start(out=xt[:, 0:nj, :], in_=xsrc), writes=[b_xt], dma=True)
            P.op('sp', lambda e: e.dma_start(out=mx[:, 0:nj, :], in_=msrc), reads=[bd['MIXd']], writes=[b_mx], dma=True)

        load3a(0)
        pn = 0
        for wn, (r0, nj) in enumerate(work):
            xt, b_xt = xts[wn % 2]
            mx, b_mx = mxs[wn % 2]
            NTK = 128 * nj
            if wn + 1 < len(work):
                load3a(wn + 1)
            for c in range(8):
                pt, b_pt = psT[c % 2]
                for j in range(nj):
                    P.op('pe', lambda e, pt=pt, mx=mx, j=j, c=c: e.matmul(pt[:, j * 128:(j + 1) * 128],
                                                                         lhsT=mx[:, j, c * 128:(c + 1) * 128], rhs=idb[:],
                                                                         start=True, stop=True),
                         reads=[b_mx, b_idb], writes=[b_pt])
                if c % 2 == 0:
                    P.op('act', lambda e, pt=pt, c=c, NTK=NTK: e.activation(out=mT[:, c, 0:NTK], in_=pt[:, 0:NTK], func=AF.Copy),
                         reads=[b_pt], writes=[b_mT])
                else:
                    P.op('dve', lambda e, pt=pt, c=c, NTK=NTK: e.tensor_copy(out=mT[:, c, 0:NTK], in_=pt[:, 0:NTK]),
                         reads=[b_pt], writes=[b_mT])
            for j in range(nj):
                for half in range(2):
                    pp, b_pp = psP[pn % 3]
                    tm, b_tm = tmps[pn % 2]
                    pn += 1
                    for c in range(8):
                        P.op('pe', lambda e, pp=pp, c=c, j=j, half=half: e.matmul(pp[:], lhsT=mT[:, c, j * 128:(j + 1) * 128],
                                                                                 rhs=wo[:, c, half * 512:(half + 1) * 512],
                                                                                 start=(c == 0), stop=(c == 7)),
                             reads=[b_mT, b_wo], writes=[b_pp])
                    P.op('dve', lambda e, pp=pp, tm=tm, half=half: e.tensor_tensor(out=tm[:], in0=pp[:],
                                                                                  in1=gA[:, half * 512:(half + 1) * 512], op=ALU.mult),
                         reads=[b_pp, b_gA], writes=[b_tm])
                    P.op('pool', lambda e, tm=tm, xt=xt, j=j, half=half: e.tensor_tensor(
                        out=xt[:, j, half * 512:(half + 1) * 512], in0=tm[:], in1=xt[:, j, half * 512:(half + 1) * 512], op=ALU.add),
                        reads=[b_tm, b_xt], writes=[b_xt])
            dst = X1d[r0:r0 + 128 * nj, :].rearrange("(j p) d -> p j d", p=128)
            P.op('sp', lambda e, xt=xt, dst=dst, nj=nj: e.dma_start(out=dst, in_=xt[:, 0:nj, :]), reads=[b_xt], dma=True)
        P.run_phase()
        if P.phase >= nph:
            return nc

    TK = 256
    NT2 = SO // TK
    with contextlib.ExitStack() as es:
        C = Ctx(nc, es)
        wu, b_wu = C.sb([128, 8, 2 * DFF], BF16)
        wd, b_wd = C.sb([128, NG, D], BF16)
        idf, b_idf = C.sb([128, 128], F32)
        gcol, b_gcol = C.sb([128, 8], F32)
        scl, b_scl = C.sb([128, 8], F32)
        gsM, b_gsM = C.sb([128, 8], F32)
        shM, b_shM = C.sb([128, 8], F32)
        gM, b_gM = C.sb([128, D], F32)
        gF, b_gF = C.sb([128, D], F32)
        cwt, b_cwt = C.sb([128, 2 * NG, 3], F32)
        cbt, b_cbt = C.sb([128, 2 * NG], F32)
        hvt, b_hvt = C.sb([128, NHQ], F32)
        x1s = [C.sb([128, 2, D], F32) for _ in range(2)]
        xs, b_xs = C.sb([128, 2, D], F32)
        junk, b_junk = C.sb([128, D], BF16)
        ss, b_ss = C.sb([128, 2], F32)
        rstd, b_rstd = C.sb([128, 2], F32)
        ss2, b_ss2 = C.sb([128, 2], F32)
        rstd2, b_rstd2 = C.sb([128, 2], F32)
        hTs3 = [C.sb([128, 8, TK], BF16) for _ in range(2)]
        ues = [C.sb([128, TK + 2], F32) for _ in range(4)]
        halo, b_halo = C.sb([128, 2 * NG, 2], F32)
        uh, b_uh = C.sb([128, 2 * NG, NHQ], F32)
        t1s = [C.sb([128, TK], F32) for _ in range(4)]
        sgt = [C.sb([128, TK], F32) for _ in range(2)]
        aT, b_aT = C.sb([128, NG, TK], BF16)
        tmps = [C.sb([128, 512], F32) for _ in range(2)]
        psT = [C.ps() for _ in range(2)]
        psU = [C.ps() for _ in range(2)]
        psD = [C.ps() for _ in range(4)]

        wuv = w_up.rearrange("(c p) n -> p c n", p=128)
        for k4 in range(4):
            lo, hi = k4 * 1376, (k4 + 1) * 1376
            P.op('pool', lambda e, lo=lo, hi=hi: e.dma_start(out=wu[:, :, lo:hi], in_=wuv[:, :, lo:hi]), writes=[b_wu], dma=True)
        P.op('pool', lambda e: e.dma_start(out=wd[:, 0:21, :], in_=w_down[0:2688, :].rearrange("(g p) n -> p g n", p=128)),
             writes=[b_wd], dma=True)
        P.op('pool', lambda e: e.dma_start(out=wd[0:64, 21, :], in_=w_down[2688:2752, :]), writes=[b_wd], dma=True)
        P.op('sp', lambda e: e.dma_start(out=idf[:], in_=c_ident), writes=[b_idf], dma=True)
        P.op('sp', lambda e: e.dma_start(out=gcol[:], in_=g_mlp), writes=[b_gcol], dma=True)
        P.op('sp', lambda e: e.dma_start(out=hvt[:], in_=hvalid), writes=[b_hvt], dma=True)
        P.op('sp', lambda e: e.dma_start(out=shM[:], in_=col_of(3 * D), allow_slow_non_contiguous=True),
             reads=[bd['mod_d']], writes=[b_shM], dma=True)
        P.op('sp', lambda e: e.dma_start(out=scl[:], in_=col_of(4 * D), allow_slow_non_contiguous=True),
             reads=[bd['mod_d']], writes=[b_scl], dma=True)
        P.op('sp', lambda e: e.dma_start(out=gM[:], in_=mod_d[0, 5 * D:6 * D].partition_broadcast(128)),
             reads=[bd['mod_d']], writes=[b_gM], dma=True)
        P.op('sp', lambda e: e.dma_start(out=gF[:], in_=g_final[0, :].partition_broadcast(128)), writes=[b_gF], dma=True)
        P.op('sp', lambda e: e.dma_start(out=cwt[:], in_=cw), writes=[b_cwt], dma=True)
        P.op('sp', lambda e: e.dma_start(out=cbt[:], in_=cb), writes=[b_cbt], dma=True)
        P.op('dve', lambda e: e.tensor_scalar(out=scl[:], in0=scl[:], scalar1=1.0, scalar2=None, op0=ALU.add),
             reads=[b_scl], writes=[b_scl])
        P.op('dve', lambda e: e.tensor_tensor(out=gsM[:], in0=scl[:], in1=gcol[:], op=ALU.mult),
             reads=[b_scl, b_gcol], writes=[b_gsM])
        b_halo_ch = [Buf(f'halo{ch}') for ch in range(2 * NG)]
        b_uh_ch = [Buf(f'uh{ch}') for ch in range(2 * NG)]
        b_aT_g = [Buf(f'aT{g}') for g in range(NG)]
        P.op('pool', lambda e: e.memset(halo[:], 0.0), writes=b_halo_ch)

        x1v = X1d[0:SO, :].rearrange("(t j p) d -> t p j d", j=2, p=128)
        ov = out.rearrange("(t j p) d -> t p j d", j=2, p=128)
        cn = {'nu': 0, 'nd': 0}

        def norm_hT(x1, b_x1, nj, hT, b_hT):
            NTK = 128 * nj
            for j in range(nj):
                P.op('act', lambda e, j=j: e.activation(out=junk[:], in_=x1[:, j, :], func=AF.Square, accum_out=ss[:, j:j + 1]),
                     reads=[b_x1], writes=[b_junk, b_ss])
            P.op('act', lambda e: e.activation(out=rstd[:, 0:nj], in_=ss[:, 0:nj], func=AF.Ln, scale=1.0 / D, bias=EPS),
                 reads=[b_ss], writes=[b_rstd])
            P.op('act', lambda e: e.activation(out=rstd[:, 0:nj], in_=rstd[:, 0:nj], func=AF.Exp, scale=-0.5),
                 reads=[b_rstd], writes=[b_rstd])
            for j in range(nj):
                P.op('act', lambda e, j=j: e.activation(out=xs[:, j, :], in_=x1[:, j, :], func=AF.Copy, scale=rstd[:, j:j + 1]),
                     reads=[b_x1, b_rstd], writes=[b_xs])
            for c in range(8):
                pt, b_pt = psT[c % 2]
                for j in range(nj):
                    P.op('pe', lambda e, pt=pt, j=j, c=c: e.transpose(out=pt[:, j * 128:(j + 1) * 128],
                                                                     in_=xs[:, j, c * 128:(c + 1) * 128], identity=idf[:]),
                         reads=[b_xs, b_idf], writes=[b_pt])
                P.op('act', lambda e, pt=pt, c=c: e.activation(out=hT[:, c, 0:NTK], in_=pt[:, 0:NTK], func=AF.Identity,
                                                              scale=gsM[:, c:c + 1], bias=shM[:, c:c + 1]),
                     reads=[b_pt, b_gsM, b_shM], writes=[b_hT])

        x1h, b_x1h = x1s[1]
        P.op('sp', lambda e: e.dma_start(out=x1h[:, 0, :], in_=X1d[SO:SO + 128, :]), reads=[bd['X1d']], writes=[b_x1h], dma=True)
        P.op('sp', lambda e: e.dma_start(out=x1s[0][0][:], in_=x1v[0]), reads=[bd['X1d']], writes=[x1s[0][1]], dma=True)
        hT, b_hT = hTs3[1]
        norm_hT(x1h, b_x1h, 1, hT, b_hT)
        for g in range(NG):
            w = 128 if g < 21 else 64
            for half in range(2):
                col0 = half * DFF + 128 * g
                ch = half * NG + g
                pu, b_pu = psU[cn['nu'] % 2]
                cn['nu'] += 1
                for c in range(8):
                    P.op('pe', lambda e, pu=pu, c=c, col0=col0, w=w, hT=hT: e.matmul(pu[0:w, 0:NHQ], lhsT=wu[:, c, col0:col0 + w],
                                                                             rhs=hT[:, c, 0:NHQ], start=(c == 0), stop=(c == 7)),
                         reads=[b_wu, b_hT], writes=[b_pu])
                P.op('dve', lambda e, pu=pu, ch=ch, w=w: e.tensor_tensor(out=uh[0:w, ch, :], in0=pu[0:w, 0:NHQ], in1=hvt[0:w, :], op=ALU.mult),
                     reads=[b_pu, b_hvt], writes=[b_uh_ch[ch]])

        for t in range(NT2):
            x1, b_x1 = x1s[t % 2]
            if t + 1 < NT2:
                P.op('sp', lambda e, t=t: e.dma_start(out=x1s[(t + 1) % 2][0][:], in_=x1v[t + 1]),
                     reads=[bd['X1d']], writes=[x1s[(t + 1) % 2][1]], dma=True)
            hT, b_hT = hTs3[t % 2]
            pend = [None, None]
            dq = []
            if t == 0:
                norm_hT(x1, b_x1, 2, hT, b_hT)
            for g in range(NG):
                w = 128 if g < 21 else 64
                res = []
                for half in range(2):
                    col0 = half * DFF + 128 * g
                    ch = half * NG + g
                    pu, b_pu = psU[cn['nu'] % 2]
                    ue, b_ue = ues[cn['nu'] % 4]
                    t1, b_t1 = t1s[cn['nu'] % 4]
                    cn['nu'] += 1
                    for c in range(8):
                        P.op('pe', lambda e, pu=pu, c=c, col0=col0, w=w, hT=hT: e.matmul(pu[0:w, 0:TK], lhsT=wu[:, c, col0:col0 + w],
                                                                                 rhs=hT[:, c, :], start=(c == 0), stop=(c == 7)),
                             reads=[b_wu, b_hT], writes=[b_pu])
                    if t % 2 == 0:
                        s_ = t // 2
                        P.op('pool', lambda e, ue=ue, ch=ch, w=w, s_=s_: e.tensor_copy(out=ue[0:w, 0:2], in_=uh[0:w, ch, 2 * s_:2 * s_ + 2]),
                             reads=[b_uh_ch[ch]], writes=[b_ue])
                    else:
                        P.op('pool', lambda e, ue=ue, ch=ch, w=w: e.tensor_copy(out=ue[0:w, 0:2], in_=halo[0:w, ch, :]),
                             reads=[b_halo_ch[ch]], writes=[b_ue])
                    P.op('act', lambda e, ue=ue, pu=pu, w=w: e.activation(out=ue[0:w, 2:TK + 2], in_=pu[0:w, 0:TK], func=AF.Copy),
                         reads=[b_pu], writes=[b_ue])
                    if t % 2 == 0:
                        P.op('pool', lambda e, ue=ue, ch=ch, w=w: e.tensor_copy(out=halo[0:w, ch, :], in_=ue[0:w, TK:TK + 2]),
                             reads=[b_ue], writes=[b_halo_ch[ch]])
                    P.op('act', lambda e, ue=ue, t1=t1, ch=ch, w=w: e.activation(
                        out=t1[0:w, :], in_=ue[0:w, 0:TK], func=AF.Identity, scale=cwt[0:w, ch, 0:1], bias=cbt[0:w, ch:ch + 1]),
                         reads=[b_ue, b_cwt, b_cbt], writes=[b_t1])
                    P.op('dve', lambda e, ue=ue, t1=t1, ch=ch, w=w: e.scalar_tensor_tensor(
                        out=t1[0:w, :], in0=ue[0:w, 1:TK + 1], scalar=cwt[0:w, ch, 1:2], in1=t1[0:w, :],
                        op0=ALU.mult, op1=ALU.add), reads=[b_ue, b_cwt, b_t1], writes=[b_t1])
                    P.op('dve', lambda e, ue=ue, t1=t1, ch=ch, w=w: e.scalar_tensor_tensor(
                        out=t1[0:w, :], in0=ue[0:w, 2:TK + 2], scalar=cwt[0:w, ch, 2:3], in1=t1[0:w, :],
                        op0=ALU.mult, op1=ALU.add), reads=[b_ue, b_cwt, b_t1], writes=[b_t1])
                    res.append((t1, b_t1))
                (tg, b_tg), (tv, b_tv) = res
                sg, b_sg = sgt[g % 2]

                def gate(sg=sg, b_sg=b_sg, tg=tg, b_tg=b_tg, tv=tv, b_tv=b_tv, g=g, w=w):
                    P.op('act', lambda e: e.activation(out=sg[0:w, :], in_=tg[0:w, :], func=AF.Silu),
                         reads=[b_tg], writes=[b_sg])
                    P.op('dve', lambda e: e.tensor_tensor(out=aT[0:w, g, :], in0=sg[0:w, :], in1=tv[0:w, :], op=ALU.mult),
                         reads=[b_sg, b_tv], writes=[b_aT_g[g]])
                def down(g=g, w=w):
                    for j in range(2):
                        for half in range(2):
                            pd, b_pd = psD[j * 2 + half]
                            P.op('pe', lambda e, pd=pd, j=j, half=half: e.matmul(
                                pd[:], lhsT=aT[0:w, g, j * 128:(j + 1) * 128], rhs=wd[0:w, g, half * 512:(half + 1) * 512],
                                start=(g == 0), stop=(g == NG - 1)), reads=[b_aT_g[g], b_wd], writes=[b_pd])
                if dq:
                    dq.pop(0)()
                if pend[0] is not None:
                    pend[0]()
                    dq.append(pend[1])
                pend[0] = gate
                pend[1] = down
            pend[0]()
            dq.append(pend[1])
            pend[0] = None
            while dq:
                dq.pop(0)()
            if t + 1 < NT2:
                norm_hT(x1s[(t + 1) % 2][0], x1s[(t + 1) % 2][1], 2, hTs3[(t + 1) % 2][0], hTs3[(t + 1) % 2][1])
            for j in range(2):
                for half in range(2):
                    pd, b_pd = psD[j * 2 + half]
                    tm, b_tm = tmps[cn['nd'] % 2]
                    cn['nd'] += 1
                    P.op('dve', lambda e, pd=pd, tm=tm, half=half: e.tensor_tensor(out=tm[:], in0=pd[:],
                                                                                  in1=gM[:, half * 512:(half + 1) * 512], op=ALU.mult),
                         reads=[b_pd, b_gM], writes=[b_tm])
                    P.op('pool', lambda e, tm=tm, x1=x1, j=j, half=half: e.tensor_tensor(
                        out=x1[:, j, half * 512:(half + 1) * 512], in0=tm[:], in1=x1[:, j, half * 512:(half + 1) * 512], op=ALU.add),
                        reads=[b_tm, b_x1], writes=[b_x1])
            for j in range(2):
                P.op('act', lambda e, x1=x1, j=j: e.activation(out=junk[:], in_=x1[:, j, :], func=AF.Square, accum_out=ss2[:, j:j + 1]),
                     reads=[b_x1], writes=[b_junk, b_ss2])
            P.op('act', lambda e: e.activation(out=rstd2[:], in_=ss2[:], func=AF.Ln, scale=1.0 / D, bias=EPS),
                 reads=[b_ss2], writes=[b_rstd2])
            P.op('act', lambda e: e.activation(out=rstd2[:], in_=rstd2[:], func=AF.Exp, scale=-0.5), reads=[b_rstd2], writes=[b_rstd2])
            for j in range(2):
                P.op('dve', lambda e, x1=x1, j=j: e.scalar_tensor_tensor(out=x1[:, j, :], in0=x1[:, j, :], scalar=rstd2[:, j:j + 1],
                                                                        in1=gF[:], op0=ALU.mult, op1=ALU.mult),
                     reads=[b_x1, b_rstd2, b_gF], writes=[b_x1])
            P.op('sp', lambda e, t=t, x1=x1: e.dma_start(out=ov[t], in_=x1[:]), reads=[b_x1], dma=True)
        P.run_phase()
    return nc


def host_consts(p, S):
    NB = S // 128
    NSL = S // 1024
    NHQ = 2 * NSL
    j = np.arange(128)[:, None]
    t = np.arange(128)[None, :]
    c = {}
    c['c_ident'] = np.eye(128, dtype=np.float32)
    c['c_triu'] = (j <= t).astype(np.float32)
    c['c_ones'] = np.ones((128, 128), np.float32)
    c['c_trisb'] = (j >= t).astype(np.float32)
    c['c_zero'] = np.zeros((128, D), np.float32)
    k = np.arange(128)[:, None]
    q = np.arange(512)[None, :]
    SO = NSL * 512
    kc = np.arange(128)[:, None]
    cc = np.arange(128)[None, :]
    trif = np.zeros((128, 2, 128), np.float32)
    tris = np.zeros((128, 2, 128), np.float32)
    trif[:, p, :] = np.where(kc <= cc, 0.0, -32768.0)
    tris[:, p, :] = np.where(kc < cc, 0.0, -32768.0)
    c['c_trif'] = trif
    c['c_tris'] = tris
    oh = np.zeros((8, S), np.float32)
    for r in range(8):
        oh[r, :] = ((np.arange(S) // 128) % 8 == r)
    c['c_onehot'] = oh
    qm = np.zeros((8, SO + 128), np.float32)
    mq = (np.arange(SO) % 512) // 128
    for r in range(8):
        qm[r, :SO] = np.where(r > 4 * p + mq, -32768.0, 0.0)
    c['c_qmask'] = qm
    c['c_ones3'] = np.ones((3, SO + 128), np.float32)
    hmb = np.zeros((128, NB, NHQ), np.float32)
    hm01 = np.zeros((128, NB, NHQ), np.float32)
    hval = np.zeros((128, NHQ), np.float32)
    tk = (np.arange(NB)[None, :] * 128 + np.arange(128)[:, None])
    for col in range(NHQ):
        i, e = divmod(col, 2)
        tq = (2 * i + p) * 512 - 2 + e
        if tq >= 0:
            hmb[:, :, col] = np.where(tk <= tq, 0.0, -32768.0)
            hm01[:, :, col] = np.where(tk < tq, 0.0, -32768.0)
            hval[:, col] = 1.0
        else:
            hmb[:, :, col] = np.where(tk == 0, 0.0, -32768.0)
            hm01[:, :, col] = -32768.0
    c['h_mbias'] = hmb
    c['h_m01'] = hm01
    c['hvalid'] = hval
    sel = np.zeros((8, 2), np.float32)
    sel[:, p] = 1.0
    c['selw'] = sel
    return c


def col128(v):
    return np.ascontiguousarray(np.asarray(v, np.float32).reshape(8, 128).T)


def make_weights(w):
    m = {}
    m['w_ada'] = np.ascontiguousarray(w['w_ada'][0])
    m['b_ada'] = np.ascontiguousarray(w['b_ada'][0].reshape(1, -1))
    m['g_attn'] = col128(w['g_attn'][0])
    m['w_in'] = np.ascontiguousarray(w['w_in'][0])
    m['bfg'] = np.ascontiguousarray(w['b_fgate'][0].reshape(1, 8))
    m['g_out'] = np.ascontiguousarray(np.concatenate([w['g_out_fox'][0], w['g_out_sb'][0]]).reshape(1, -1))
    m['w_out'] = np.ascontiguousarray(w['w_out'][0])
    m['g_mlp'] = col128(w['g_mlp'][0])
    m['w_up'] = np.ascontiguousarray(w['w_up'][0])
    cwf = np.asarray(w['conv_w'][0], np.float32)
    cbf = np.asarray(w['conv_b'][0], np.float32)
    cw = np.zeros((128, 2 * NG, 3), np.float32)
    cb = np.zeros((128, 2 * NG), np.float32)
    for half in range(2):
        for g in range(NG):
            wdt = 128 if g < 21 else 64
            lo = half * DFF + 128 * g
            cw[0:wdt, half * NG + g, :] = cwf[:, lo:lo + wdt].T
            cb[0:wdt, half * NG + g] = cbf[lo:lo + wdt]
    m['cw'] = cw
    m['cb'] = cb
    m['w_down'] = np.ascontiguousarray(w['w_down'][0])
    m['g_final'] = np.ascontiguousarray(np.asarray(w['g_final'], np.float32).reshape(1, -1))
    return m


_NC_CACHE = {}


def run(x, c, w, n_cores=8):
    import os
    x = np.asarray(x, np.float32)
    c = np.asarray(c, np.float32)
    B, S, _ = x.shape
    NSL = S // 1024
    SO = NSL * 512
    if S not in _NC_CACHE:
        _NC_CACHE[S] = build(S, int(os.environ.get("MK_NPH", "99")))
    nc = _NC_CACHE[S]
    wm = make_weights(w)
    consts = [host_consts(p, S) for p in range(2)]
    in_maps = []
    for k in range(n_cores):
        b, p = (k // 2) % B, k % 2
        m = dict(wm)
        m.update(consts[p])
        m['x'] = np.ascontiguousarray(x[b])
        m['ccol'] = col128(c[b])
        xo = np.zeros((SO + 128, D), np.float32)
        for i in range(NSL):
            T = 2 * i + p
            xo[512 * i:512 * (i + 1)] = x[b, 512 * T:512 * (T + 1)]
            if T > 0:
                xo[SO + 2 * i:SO + 2 * i + 2] = x[b, 512 * T - 2:512 * T]
        m['xo'] = xo
        in_maps.append(m)
    res = run_bass_kernel_spmd(nc, in_maps, core_ids=list(range(n_cores)))
    if os.environ.get("MK_DBG"):
        global DBG
        DBG = [{k_: np.asarray(v) for k_, v in r.items()} for r in res.results[:2]]
    outv = np.zeros((B, S, D), np.float32)
    for k in range(n_cores):
        b, p = k // 2, k % 2
        if b >= B:
            continue
        o = np.asarray(res.results[k]["out"], np.float32).reshape(SO, D)
        for i in range(NSL):
            T = 2 * i + p
            outv[b, 512 * T:512 * (T + 1)] = o[512 * i:512 * (i + 1)]
    return outv


def kernel(x, c, w_ada, b_ada, g_attn, w_in, b_fgate, g_out_fox, g_out_sb, w_out,
           g_mlp, w_up, conv_w, conv_b, w_down, g_final):
    w = dict(w_ada=np.asarray(w_ada, np.float32), b_ada=np.asarray(b_ada, np.float32),
             g_attn=np.asarray(g_attn, np.float32), w_in=np.asarray(w_in, np.float32),
             b_fgate=np.asarray(b_fgate, np.float32), g_out_fox=np.asarray(g_out_fox, np.float32),
             g_out_sb=np.asarray(g_out_sb, np.float32), w_out=np.asarray(w_out, np.float32),
             g_mlp=np.asarray(g_mlp, np.float32), w_up=np.asarray(w_up, np.float32),
             conv_w=np.asarray(conv_w, np.float32), conv_b=np.asarray(conv_b, np.float32),
             w_down=np.asarray(w_down, np.float32), g_final=np.asarray(g_final, np.float32))
    return run(x, c, w, n_cores=8)
```
